# Optimizing a Trainium2 kernel written in Bass

```python
import jax
import jax.numpy as jnp
from jax import lax
import numpy as np

D_MODEL = 2048
BATCH = 16
SEQ = 256
DEPTH = 2
DEC_BATCH = 8
DEC_SEQ = 4096
PAST_LEN = 512

GRID_W = 64
HEAD_DIM = 128
A_HEADS = 8
A_KV_HEADS = 2
A_GROUP = A_HEADS // A_KV_HEADS
A_WIDTH = A_HEADS * HEAD_DIM
A_KV_WIDTH = A_KV_HEADS * HEAD_DIM
WINDOW = 128
ATT_BLOCK = 128
ROPE_THETA = 10000.0
MASK_VALUE = -1e30
B_HEADS = 4
B_DK = 128
B_DV = 128
B_KEY_WIDTH = B_HEADS * B_DK
B_WIDTH = B_HEADS * B_DV
B_CHUNK = 32
C_GROUPS = 4
C_CHUNK = 128
C_WIDTH = D_MODEL // 4
C_GROUP_DIM = C_WIDTH // C_GROUPS
MIX_WIDTH = A_WIDTH + B_WIDTH + C_WIDTH
IN_SIZES = (A_WIDTH, A_KV_WIDTH, A_KV_WIDTH, A_WIDTH,
            B_KEY_WIDTH, B_KEY_WIDTH, B_KEY_WIDTH, B_WIDTH, B_WIDTH,
            C_WIDTH, C_WIDTH, C_WIDTH)
IN_COLS = 2 * A_WIDTH + 2 * A_KV_WIDTH + 3 * B_KEY_WIDTH + 2 * B_WIDTH + 3 * C_WIDTH
EPS = 1e-6

kernel_name = 'hybrid_diffusion_parallel_groups_step'


def rms_norm(x, g):
    xf = x.astype(jnp.float32)
    y = xf * lax.rsqrt(jnp.mean(xf * xf, axis=-1, keepdims=True) + EPS)
    return (y * g.astype(jnp.float32)).astype(x.dtype)


def layer_norm(x, g, b):
    xf = x.astype(jnp.float32)
    xc = xf - jnp.mean(xf, axis=-1, keepdims=True)
    y = xc * lax.rsqrt(jnp.mean(xc * xc, axis=-1, keepdims=True) + EPS)
    return (y * g.astype(jnp.float32) + b.astype(jnp.float32)).astype(x.dtype)


def adaln(cvec, w, b):
    m = jax.nn.silu(cvec) @ w + b
    return jnp.split(m, 3, axis=-1)


def split_proj(p):
    parts, start = [], 0
    for size in IN_SIZES:
        parts.append(p[..., start:start + size])
        start += size
    return parts


def axial_rope_tables(T):
    n_rows = T // GRID_W
    row = jnp.repeat(jnp.arange(n_rows), GRID_W).astype(jnp.float32)
    col = jnp.tile(jnp.arange(GRID_W), n_rows).astype(jnp.float32)
    half = HEAD_DIM // 2
    freq = ROPE_THETA ** (-jnp.arange(0, half, 2, dtype=jnp.float32) / half)
    ang = jnp.concatenate([row[:, None] * freq, col[:, None] * freq], axis=-1)
    return jnp.cos(ang), jnp.sin(ang)


def apply_axial_rope(x, cos, sin):
    T = x.shape[1]
    q4 = HEAD_DIM // 4
    xf = x.astype(jnp.float32).reshape(x.shape[:-1] + (2, 2, q4))
    x1 = xf[..., 0, :]
    x2 = xf[..., 1, :]
    cs = cos.reshape(T, 2, q4)[None, :, None]
    sn = sin.reshape(T, 2, q4)[None, :, None]
    out = jnp.stack([x1 * cs - x2 * sn, x2 * cs + x1 * sn], axis=-2)
    return out.reshape(x.shape).astype(x.dtype)


def sink_attend(qb, keys, vals, mask, sink):
    f32 = jnp.float32
    s = jnp.einsum('bhgqd,bhkd->bhgqk', qb.astype(f32), keys.astype(f32)) * (HEAD_DIM ** -0.5)
    if mask is not None:
        s = jnp.where(mask, s, MASK_VALUE)
    sk = sink.astype(f32).reshape(A_KV_HEADS, A_GROUP)[None, :, :, None, None]
    m = jnp.maximum(jnp.max(s, axis=-1, keepdims=True), sk)
    p = jnp.exp(s - m)
    denom = jnp.sum(p, axis=-1, keepdims=True) + jnp.exp(sk - m)
    out = jnp.einsum('bhgqk,bhkd->bhgqd', p / denom, vals.astype(f32))
    return out.astype(qb.dtype)


def query_blocks(q):
    Bn, T = q.shape[:2]
    return q.reshape(Bn, T // ATT_BLOCK, ATT_BLOCK, A_KV_HEADS, A_GROUP, HEAD_DIM).transpose(1, 0, 3, 4, 2, 5)


def merge_blocks(o, Bn, T):
    return o.transpose(1, 0, 4, 2, 3, 5).reshape(Bn, T, A_WIDTH)


def ctx_attention(q, k, v, sink):
    Bn, L = q.shape[:2]
    out = lax.map(lambda qi: sink_attend(qi, k, v, None, sink), query_blocks(q))
    return merge_blocks(out, Bn, L)


def latent_attention(q, k, v, k_ctx, v_ctx, sink):
    Bn, T = q.shape[:2]
    nblk = T // ATT_BLOCK
    Lc = k_ctx.shape[2]
    pad = ((0, 0), (0, 0), (ATT_BLOCK, ATT_BLOCK), (0, 0))
    kp = jnp.pad(k, pad)
    vp = jnp.pad(v, pad)
    q_off = jnp.arange(ATT_BLOCK)
    k_off = jnp.arange(3 * ATT_BLOCK) - ATT_BLOCK
    band = jnp.abs(k_off[None, :] - q_off[:, None]) <= WINDOW
    ctx_mask = jnp.ones((ATT_BLOCK, Lc), dtype=bool)
    k_ctx = k_ctx.astype(k.dtype)
    v_ctx = v_ctx.astype(v.dtype)

    def block(args):
        j, qi = args
        kb = lax.dynamic_slice_in_dim(kp, j * ATT_BLOCK, 3 * ATT_BLOCK, axis=2)
        vb = lax.dynamic_slice_in_dim(vp, j * ATT_BLOCK, 3 * ATT_BLOCK, axis=2)
        kpos = j * ATT_BLOCK + k_off
        valid = band & ((kpos >= 0) & (kpos < T))[None, :]
        keys = jnp.concatenate([kb, k_ctx], axis=2)
        vals = jnp.concatenate([vb, v_ctx], axis=2)
        mask = jnp.concatenate([valid, ctx_mask], axis=1)
        return sink_attend(qi, keys, vals, mask, sink)

    out = lax.map(block, (jnp.arange(nblk), query_blocks(q)))
    return merge_blocks(out, Bn, T)


def qkv_heads(aq, ak, av, qg, kg, rope):
    Bn, T = aq.shape[:2]
    q = rms_norm(aq.reshape(Bn, T, A_HEADS, HEAD_DIM), qg)
    k = rms_norm(ak.reshape(Bn, T, A_KV_HEADS, HEAD_DIM), kg)
    if rope is not None:
        q = apply_axial_rope(q, *rope)
        k = apply_axial_rope(k, *rope)
    v = av.reshape(Bn, T, A_KV_HEADS, HEAD_DIM)
    return q, k.transpose(0, 2, 1, 3), v.transpose(0, 2, 1, 3)


def hgrn_scan(q, k, v, logf, s0):
    Bn, T = q.shape[:2]
    nC = T // B_CHUNK

    def to_chunks(a):
        return a.reshape(Bn, nC, B_CHUNK, B_HEADS, a.shape[-1]).transpose(1, 0, 3, 2, 4)

    causal = jnp.tril(jnp.ones((B_CHUNK, B_CHUNK), dtype=bool))

    def step(S, xs):
        qc, kc, vc, lfc = xs
        b = jnp.cumsum(lfc, axis=-2)
        qe = qc * jnp.exp(b)
        ke = kc * jnp.exp(-b)
        att = jnp.where(causal, jnp.einsum('bhtd,bhsd->bhts', qe, ke), 0.0)
        o = jnp.einsum('bhtd,bhdv->bhtv', qe, S) + jnp.einsum('bhts,bhsv->bhtv', att, vc)
        b_last = b[..., -1:, :]
        S = jnp.exp(b_last)[..., 0, :, None] * S + jnp.einsum('bhsd,bhsv->bhdv', kc * jnp.exp(b_last - b), vc)
        return S, o

    s_end, o = lax.scan(step, s0, (to_chunks(q), to_chunks(k), to_chunks(v), to_chunks(logf)))
    o = o.transpose(1, 0, 3, 2, 4).reshape(Bn, T, B_HEADS, B_DV)
    return o, s_end


def hgrn_mixer(bq, bff, bfb, bi, bg, lb, norm_g, s0_f, s0_b):
    f32 = jnp.float32
    Bn, T = bq.shape[:2]

    def heads(a, d):
        return a.astype(f32).reshape(Bn, T, B_HEADS, d)

    q = jax.nn.silu(heads(bq, B_DK))
    v = heads(bi, B_DV)
    lbh = lb.astype(f32).reshape(2, B_HEADS, B_DK)

    def gates(z, lbd):
        logf = jax.nn.log_sigmoid(z) + jnp.log1p(lbd * jnp.exp(-z))
        return logf, (1.0 - lbd) * jax.nn.sigmoid(-z)

    lf_f, k_f = gates(heads(bff, B_DK), lbh[0])
    lf_b, k_b = gates(heads(bfb, B_DK), lbh[1])
    o_f, s_f = hgrn_scan(q, k_f, v, lf_f, s0_f.astype(f32))

    def rev(a):
        return jnp.flip(a, axis=1)

    o_b, s_b = hgrn_scan(rev(q), rev(k_b), rev(v), rev(lf_b), s0_b.astype(f32))
    o = rms_norm(o_f + rev(o_b), norm_g).reshape(Bn, T, B_WIDTH)
    return o.astype(bg.dtype) * jax.nn.silu(bg), s_f, s_b


def sgu_mixer(cu, cv, cg, ln_g, ln_b, w_s, b_s):
    Bn, T = cu.shape[:2]
    vn = layer_norm(cv, ln_g, ln_b).reshape(Bn, T // C_CHUNK, C_CHUNK, C_GROUPS, C_GROUP_DIM)
    s = jnp.einsum('gpq,bnqgc->bnpgc', w_s, vn) + jnp.transpose(b_s)[None, None, :, :, None]
    return cu * s.reshape(Bn, T, C_WIDTH) * jax.nn.silu(cg)


def layer_step(x, cvec, p, rope, attend, s0_f, s0_b):
    norm_g, w_ada, b_ada, w_in, qg, kg, lb, hg, lng, lnb, ws, bs, w_out = p
    shift, scale, gate = adaln(cvec, w_ada, b_ada)
    h = rms_norm(x, norm_g) * (1.0 + scale) + shift
    aq, ak, av, ag, bq, bff, bfb, bi, bg, cu, cv, cg = split_proj(h @ w_in)
    q, k, v = qkv_heads(aq, ak, av, qg, kg, rope)
    a_out = attend(q, k, v) * jax.nn.silu(ag)
    b_out, s_f, s_b = hgrn_mixer(bq, bff, bfb, bi, bg, lb, hg, s0_f, s0_b)
    c_out = sgu_mixer(cu, cv, cg, lng, lnb, ws, bs)
    y = jnp.concatenate([a_out, b_out, c_out], axis=-1) @ w_out
    return x + gate * y, k, v, s_f, s_b


def setup_inputs(seed: int = 0) -> dict:
    key = jax.random.key(seed)
    ks = jax.random.split(key, 21)

    def n(k, s):
        return jax.random.normal(k, s, jnp.float32)

    return {
        'x_prompt': n(ks[0], (BATCH, SEQ, D_MODEL)),
        'x_sample': n(ks[1], (DEC_BATCH, DEC_SEQ, D_MODEL)),
        'cache_k': n(ks[2], (DEC_BATCH, DEPTH, A_KV_HEADS, PAST_LEN, HEAD_DIM)),
        'cache_v': n(ks[3], (DEC_BATCH, DEPTH, A_KV_HEADS, PAST_LEN, HEAD_DIM)),
        'state_hgrn': 0.3 * n(ks[4], (DEC_BATCH, DEPTH, 2, B_HEADS, B_DK, B_DV)),
        'c': n(ks[5], (DEC_BATCH, D_MODEL)),
        'c_ctx': n(ks[6], (D_MODEL,)),
        'norm_g': 1.0 + 0.05 * n(ks[7], (DEPTH, D_MODEL)),
        'w_ada': (0.2 * D_MODEL ** -0.5) * n(ks[8], (DEPTH, D_MODEL, 3 * D_MODEL)),
        'b_ada': 0.02 * n(ks[9], (DEPTH, 3 * D_MODEL)),
        'w_in': (D_MODEL ** -0.5) * n(ks[10], (DEPTH, D_MODEL, IN_COLS)),
        'q_norm_g': 1.0 + 0.05 * n(ks[11], (DEPTH, HEAD_DIM)),
        'k_norm_g': 1.0 + 0.05 * n(ks[12], (DEPTH, HEAD_DIM)),
        'attn_sink': 0.5 * n(ks[13], (DEPTH, A_HEADS)),
        'hgrn_lb': n(ks[14], (DEPTH, 2, B_KEY_WIDTH)),
        'hgrn_norm_g': 1.0 + 0.05 * n(ks[15], (DEPTH, B_DV)),
        'sgu_norm_g': 1.0 + 0.05 * n(ks[16], (DEPTH, C_WIDTH)),
        'sgu_norm_b': 0.02 * n(ks[17], (DEPTH, C_WIDTH)),
        'sgu_w': (C_CHUNK ** -0.5) * n(ks[18], (DEPTH, C_GROUPS, C_CHUNK, C_CHUNK)),
        'sgu_b': 1.0 + 0.05 * n(ks[19], (DEPTH, C_GROUPS, C_CHUNK)),
        'w_out': (MIX_WIDTH ** -0.5) * n(ks[20], (DEPTH, MIX_WIDTH, D_MODEL)),
    }


def reference(x_prompt, x_sample, cache_k, cache_v, state_hgrn, c, c_ctx, norm_g, w_ada, b_ada, w_in,
              q_norm_g, k_norm_g, attn_sink, hgrn_lb, hgrn_norm_g, sgu_norm_g, sgu_norm_b, sgu_w, sgu_b, w_out):
    lb_p = jax.nn.softmax(hgrn_lb.astype(jnp.float32), axis=0)
    lb_all = jnp.cumsum(lb_p, axis=0) - lb_p[0:1]
    rope = axial_rope_tables(x_sample.shape[1])
    ctx_cond = c_ctx[None, None, :]
    lat_cond = c[:, None, :]
    s_zero = jnp.zeros((x_prompt.shape[0], B_HEADS, B_DK, B_DV), jnp.float32)
    xp = x_prompt
    xs = x_sample
    ks_out, vs_out, ss_out = [], [], []
    for l in range(DEPTH):
        p = (norm_g[l], w_ada[l], b_ada[l], w_in[l], q_norm_g[l], k_norm_g[l], lb_all[l], hgrn_norm_g[l],
             sgu_norm_g[l], sgu_norm_b[l], sgu_w[l], sgu_b[l], w_out[l])
        sink = attn_sink[l]
        xp, k_c, v_c, sf_c, sb_c = layer_step(
            xp, ctx_cond, p, None, lambda q, k, v: ctx_attention(q, k, v, sink), s_zero, s_zero)
        ks_out.append(k_c)
        vs_out.append(v_c)
        ss_out.append(jnp.stack([sf_c, sb_c], axis=1).astype(xp.dtype))
        kc_l = cache_k[:, l]
        vc_l = cache_v[:, l]
        xs, _, _, _, _ = layer_step(
            xs, lat_cond, p, rope,
            lambda q, k, v: latent_attention(q, k, v, kc_l, vc_l, sink),
            state_hgrn[:, l, 0], state_hgrn[:, l, 1])
    y_prompt = xp
    y_sample = xs
    new_cache_k = jnp.stack(ks_out, axis=1)
    new_cache_v = jnp.stack(vs_out, axis=1)
    new_state_hgrn = jnp.stack(ss_out, axis=1)
    return (y_prompt, y_sample, new_cache_k, new_cache_v, new_state_hgrn)
```

```python
import numpy as np
import ml_dtypes
from contextlib import ExitStack
import concourse.bass as bass
import concourse.mybir as mybir
from concourse.bass_utils import run_bass_kernel_spmd

F32 = mybir.dt.float32
BF16 = mybir.dt.bfloat16
AF = mybir.ActivationFunctionType
ALU = mybir.AluOpType

D = 2048
DEPTH = 2
NCOL = 6656
EPS = 1e-6
SEG = 512
ATT_SCALE = 128 ** -0.5


class V:
    __slots__ = ("ap", "keys")

    def __init__(self, ap, keys):
        self.ap = ap
        self.keys = tuple(keys)

    def __getitem__(self, idx):
        return V(self.ap[idx], self.keys)

    def re(self, s, **kw):
        return V(self.ap.rearrange(s, **kw), self.keys)

    def bc(self, shape):
        return V(self.ap.to_broadcast(list(shape)), self.keys)

    def bitcast(self, dt):
        return V(self.ap.bitcast(dt), self.keys)

    def k(self, *suffix):
        return V(self.ap, [self.keys[0] + tuple(suffix)])


class Buf:
    def __init__(self, handle, name):
        self.h = handle
        self.name = name

    def __getitem__(self, idx):
        return V(self.h[idx], [(self.name,)])


class Op:
    __slots__ = ("eng", "fn", "deps", "is_dma", "dkey", "grp", "needs_sig", "sig", "waits", "idx", "cnt")


class Group:
    __slots__ = ("key", "final", "last_op")


COMPUTE = ("pe", "act", "dve", "pool")


class Prog:
    def __init__(self, nc, es, max_dma_sems=80):
        self.nc = nc
        self.es = es
        self.eng_sem = {e: es.enter_context(nc.semaphore("sem_" + e)) for e in COMPUTE}
        self.sig_count = {e: 0 for e in COMPUTE}
        self.dma_sem = {}
        self.dma_count = {}
        self.dma_prev_group = {}
        self.max_dma_sems = max_dma_sems
        self.carry = {}
        self.dead = False
        self.reset_block()

    def reset_block(self):
        self.ops = []
        self.last_w = dict(self.carry)
        self.readers = {}
        self.block_dma_ops = []

    def _sem_for(self, key):
        if key not in self.dma_sem:
            assert len(self.dma_sem) < self.max_dma_sems, "too many DMA semaphores"
            self.dma_sem[key] = self.es.enter_context(self.nc.semaphore("dsem%d" % len(self.dma_sem)))
            self.dma_count[key] = 0
        return self.dma_sem[key]

    def group(self, key):
        g = Group()
        g.key = key
        g.final = None
        g.last_op = None
        self._sem_for(key)
        return g

    def add(self, eng, fn, reads=(), writes=(), dma_key=None, group=None, carry=False):
        if self.dead:
            return None
        op = Op()
        op.eng = eng
        op.fn = fn
        op.is_dma = dma_key is not None or group is not None
        op.needs_sig = False
        op.sig = None
        op.grp = None
        op.dkey = None
        op.idx = len(self.ops)
        deps = []
        for r in reads:
            w = self.last_w.get(r)
            if w is not None:
                deps.append((w, "raw"))
            if r[0].startswith("ps") and len(r) == 1:
                for rd in self.readers.get(r, {}).values():
                    if rd.eng != eng:
                        deps.append((rd, "raw"))
        for wkey in writes:
            w = self.last_w.get(wkey)
            if w is not None:
                deps.append((w, "waw"))
            for rd in self.readers.get(wkey, {}).values():
                deps.append((rd, "war"))
        if op.is_dma:
            if group is None:
                group = self.group(dma_key)
            key = group.key
            op.dkey = key
            op.grp = group
            prev = self.dma_prev_group.get(key)
            if prev is not None and prev is not group and prev.last_op is not None:
                deps.append((prev.last_op, "ser"))
            deps = [(d_, k_) for (d_, k_) in deps if not (d_.is_dma and d_.grp is group)]
            self.dma_count[key] += 16
            group.final = self.dma_count[key]
            group.last_op = op
            self.dma_prev_group[key] = group
            self.block_dma_ops.append(op)
        op.deps = deps
        for r in reads:
            self.readers.setdefault(r, {})[("dma", op.idx) if op.is_dma else op.eng] = op
        for wkey in writes:
            self.last_w[wkey] = op
            self.readers[wkey] = {}
            if carry:
                self.carry[wkey] = op
            elif wkey in self.carry:
                del self.carry[wkey]
        self.ops.append(op)
        return op

    def _plan(self):
        for op in self.ops:
            eff = []
            for d, kind in op.deps:
                if d is op:
                    continue
                if (not d.is_dma) and (not op.is_dma) and d.eng == op.eng:
                    if op.eng == "pe":
                        continue
                    if kind != "raw":
                        continue
                eff.append(d)
                if not d.is_dma:
                    d.needs_sig = True
            op.deps = eff
        for op in self.ops:
            if op.is_dma:
                continue
            if op.needs_sig and op.sig is None:
                self.sig_count[op.eng] += 1
                op.sig = self.sig_count[op.eng]
        eng_vc = {}
        op_vc = {}
        for op in self.ops:
            vc = eng_vc.setdefault(op.eng, {})
            waits = []
            for d in op.deps:
                if d.is_dma:
                    sem, val = ("d", d.dkey), d.grp.final
                else:
                    sem, val = ("e", d.eng), d.sig
                if vc.get(sem, 0) >= val:
                    continue
                waits.append((sem, val))
                dvc = op_vc.get(id(d))
                if dvc is not None:
                    for s2, v2 in dvc.items():
                        if vc.get(s2, 0) < v2:
                            vc[s2] = v2
                if vc.get(sem, 0) < val:
                    vc[sem] = val
            ww = {}
            for s, v in waits:
                if ww.get(s, 0) < v:
                    ww[s] = v
            op.waits = list(ww.items())
            if op.is_dma:
                nv = dict(vc)
                nv[("d", op.dkey)] = op.grp.final
                op_vc[id(op)] = nv
            elif op.needs_sig:
                nv = dict(vc)
                nv[("e", op.eng)] = op.sig
                op_vc[id(op)] = nv

    def _check_deadlock(self, by_eng):
        if not hasattr(self, "sim"):
            self.sim = {}
        sim = self.sim
        ptr = {e: 0 for e in by_eng}
        progress = True
        while progress:
            progress = False
            for e, lst in by_eng.items():
                while ptr[e] < len(lst):
                    op = lst[ptr[e]]
                    if any(sim.get(sem, 0) < val for sem, val in op.waits):
                        break
                    if op.is_dma:
                        k_ = ("d", op.dkey)
                        sim[k_] = sim.get(k_, 0) + 16
                    elif op.needs_sig:
                        k_ = ("e", op.eng)
                        sim[k_] = sim.get(k_, 0) + 1
                        assert sim[k_] == op.sig, (k_, sim[k_], op.sig)
                    ptr[e] += 1
                    progress = True
        stuck = {e: ptr[e] for e in by_eng if ptr[e] < len(by_eng[e])}
        if stuck:
            msg = []
            for e, i in stuck.items():
                op = by_eng[e][i]
                msg.append("%s stuck at op %d/%d waits=%s have=%s" % (
                    e, i, len(by_eng[e]), op.waits, [(sm, sim.get(sm, 0)) for sm, _ in op.waits]))
            raise RuntimeError("DEADLOCK: " + " | ".join(msg))

    def emit_block(self, final_wait=True, skip_final=()):
        if self.dead:
            return
        if final_wait:
            seen = {}
            for op in self.block_dma_ops:
                if op.dkey in skip_final:
                    continue
                seen[(op.eng, op.dkey)] = op
            for eng in sorted(set(k_[0] for k_ in seen)):
                fin = self.add(eng, lambda e: None, reads=(), writes=())
                fin.deps = [(o_, "raw") for (e_, _), o_ in seen.items() if e_ == eng]
        self._plan()
        by_eng = {}
        for op in self.ops:
            by_eng.setdefault(op.eng, []).append(op)
        self._check_deadlock(by_eng)
        nc = self.nc
        prog = self

        def run(eng_name, e):
            for op in by_eng.get(eng_name, ()):
                for sem, val in op.waits:
                    if sem[0] == "d":
                        e.wait_ge(prog.dma_sem[sem[1]], val)
                    else:
                        e.wait_ge(prog.eng_sem[sem[1]], val)
                inst = op.fn(e)
                if op.is_dma:
                    inst.then_inc(prog.dma_sem[op.dkey], 16)
                elif op.needs_sig:
                    assert inst is not None
                    inst.then_inc(prog.eng_sem[op.eng], 1)

        with nc.Block() as block:
            @block.tensor
            def _(e):
                run("pe", e)

            @block.scalar
            def _(e):
                run("act", e)

            @block.vector
            def _(e):
                run("dve", e)

            @block.gpsimd
            def _(e):
                run("pool", e)

            @block.sync
            def _(e):
                run("sp", e)
        self.reset_block()


def _k(*vs):
    out = []
    for v in vs:
        if isinstance(v, V):
            out.extend(v.keys)
    return out


def _ap(x):
    return x.ap if isinstance(x, V) else x


class Ops:
    def __init__(self, P):
        self.P = P

    def mm(self, out, lhsT, rhs, start=True, stop=True):
        self.P.add("pe", lambda e: e.matmul(out.ap, lhsT=lhsT.ap, rhs=rhs.ap, start=start, stop=stop),
                   reads=_k(lhsT, rhs), writes=_k(out))

    def tr(self, out, in_, ident):
        self.P.add("pe", lambda e: e.transpose(out.ap, in_.ap, ident.ap), reads=_k(in_, ident), writes=_k(out))

    def act(self, out, in_, func, bias=None, scale=None, accum=None):
        kw = {}
        if bias is not None:
            kw["bias"] = _ap(bias)
        if scale is not None:
            kw["scale"] = _ap(scale)
        if accum is not None:
            kw["accum_out"] = accum.ap
        self.P.add("act", lambda e: e.activation(out.ap, in_.ap, func, **kw),
                   reads=_k(in_, bias, scale), writes=_k(out, accum))

    def tt(self, eng, out, in0, in1, op):
        self.P.add(eng, lambda e: e.tensor_tensor(out.ap, in0.ap, in1.ap, op), reads=_k(in0, in1), writes=_k(out))

    def ts(self, eng, out, in0, s1, s2, op0, op1):
        self.P.add(eng, lambda e: e.tensor_scalar(out.ap, in0.ap, _ap(s1), _ap(s2), op0, op1),
                   reads=_k(in0, s1, s2), writes=_k(out))

    def stt(self, eng, out, in0, scalar, in1, op0, op1):
        self.P.add(eng, lambda e: e.scalar_tensor_tensor(out.ap, in0.ap, _ap(scalar), in1.ap, op0, op1),
                   reads=_k(in0, scalar, in1), writes=_k(out))

    def copy(self, eng, out, in_):
        if eng == "act":
            self.P.add("act", lambda e: e.activation(out.ap, in_.ap, AF.Copy), reads=_k(in_), writes=_k(out))
        else:
            self.P.add(eng, lambda e: e.tensor_copy(out.ap, in_.ap), reads=_k(in_), writes=_k(out))

    def recip(self, out, in_):
        self.P.add("dve", lambda e: e.reciprocal(out.ap, in_.ap), reads=_k(in_), writes=_k(out))

    def scan(self, out, d0, d1, init=0.0):
        self.P.add("dve", lambda e: e.tensor_tensor_scan(out.ap, d0.ap, d1.ap, init, ALU.mult, ALU.add),
                   reads=_k(d0, d1), writes=_k(out))

    def memset(self, eng, out, val):
        self.P.add(eng, lambda e: e.memset(out.ap, val), reads=(), writes=_k(out))

    def bn_stats(self, out, in_):
        self.P.add("dve", lambda e: e.bn_stats(out.ap, in_.ap), reads=_k(in_), writes=_k(out))

    def bn_aggr(self, out, in_):
        self.P.add("dve", lambda e: e.bn_aggr(out.ap, in_.ap), reads=_k(in_), writes=_k(out))

    def dma(self, q, out, in_, key, group=None, carry=False, **kw):
        self.P.add(q, lambda e: e.dma_start(out=out.ap, in_=in_.ap, **kw), reads=_k(in_), writes=_k(out),
                   dma_key=key, group=group, carry=carry)


class StopBuild(Exception):
    pass


OPTS = {"stop": None}


def maybe_stop(P, tag):
    if OPTS["stop"] == tag and not P.dead:
        P.emit_block()
        P.dead = True


class Ring:
    def __init__(self, views):
        self.views = views
        self.i = 0

    def get(self):
        v = self.views[self.i % len(self.views)]
        self.i += 1
        return v


def w_in_perm():
    aq, ak, av, ag = 0, 1024, 1280, 1536
    bq, bff, bfb, bi, bg = 2560, 3072, 3584, 4096, 4608
    cu, cv, cg = 5120, 5632, 6144
    r = lambda s, n: list(range(s, s + n))
    cols = []
    cols += r(ak, 256) + r(av, 256)
    cols += r(aq, 1024) + r(ag, 1024)
    cols += r(cv, 512) + r(cg, 512) + r(cu, 512)
    cols += r(bi, 512)
    for h in range(4):
        cols += r(bq + 128 * h, 128) + r(bff + 128 * h, 128) + r(bfb + 128 * h, 128) + r(bg + 128 * h, 128)
    assert len(cols) == NCOL and len(set(cols)) == NCOL
    return np.array(cols, dtype=np.int64)


def host_consts(TS):
    c = {}
    c["ident_f"] = np.eye(128, dtype=np.float32)
    c["ident_b"] = np.eye(128, dtype=np.float32).astype(ml_dtypes.bfloat16)
    rm = np.zeros((128, 128), np.float32)
    for base in (0, 64):
        for f in range(32):
            rm[base + 32 + f, base + f] = -1.0
            rm[base + f, base + 32 + f] = 1.0
    c["rot_m"] = rm.astype(ml_dtypes.bfloat16)
    c["ones_b"] = np.ones((128, 128), np.float32).astype(ml_dtypes.bfloat16)
    c["ones_f"] = np.ones((128, 128), np.float32)
    kk = np.arange(128)[:, None]
    qq = np.arange(128)[None, :]
    mp = (kk >= qq).astype(np.float32)
    mn = (kk <= qq).astype(np.float32)
    c["mask_prev"] = np.tile(mp, (1, 4)).astype(ml_dtypes.bfloat16)
    c["mask_next"] = np.tile(mn, (1, 4)).astype(ml_dtypes.bfloat16)
    same = (kk // 32) == (qq // 32)
    hf = (same & (kk <= qq)).astype(np.float32)
    hb = (same & (kk >= qq)).astype(np.float32)
    c["hmask_f"] = np.tile(hf, (1, 4)).astype(np.float32)
    c["hmask_b"] = np.tile(hb, (1, 4)).astype(np.float32)
    sm = np.ones((128, 512), np.float32)
    sm[:, ::32] = 0.0
    c["scanmask"] = sm
    cm = np.zeros((128, 4), np.float32)
    for ch in range(4):
        cm[32 * ch:32 * ch + 32, ch] = 1.0
    c["cmask"] = cm
    t = np.arange(TS)
    row = (t // 64).astype(np.float32)
    col = (t % 64).astype(np.float32)
    half = 64
    freq = (10000.0 ** (-np.arange(0, half, 2, dtype=np.float32) / half)).astype(np.float32)
    ang_r = row[None, :] * freq[:, None]
    ang_c = col[None, :] * freq[:, None]
    cosT = np.concatenate([np.cos(ang_r), np.cos(ang_r), np.cos(ang_c), np.cos(ang_c)], 0).astype(np.float32)
    sinT = np.concatenate([np.sin(ang_r), np.sin(ang_r), np.sin(ang_c), np.sin(ang_c)], 0).astype(np.float32)
    c["cosT"] = cosT
    c["sinT"] = sinT
    return c


CONST_SPECS = {
    "ident_f": ([128, 128], F32), "ident_b": ([128, 128], BF16), "rot_m": ([128, 128], BF16),
    "ones_b": ([128, 128], BF16), "ones_f": ([128, 128], F32),
    "mask_prev": ([128, 512], BF16), "mask_next": ([128, 512], BF16),
    "hmask_f": ([128, 512], F32), "hmask_b": ([128, 512], F32),
    "scanmask": ([128, 512], F32), "cmask": ([128, 4], F32),
}


def adaln_gen(o, l, I, S, C, ADA, wa, accb, get_bank, rowsT, t0):
    mod, g1, badaT, ngT = ADA["mod"], ADA["g1"], ADA["badaT"], ADA["ngT"]
    scv = ADA["sct_b"][:, :].re("p (c k) -> p c k", c=2)

    def dv(ap, *key):
        return V(ap, [("dram",) + tuple(key)])
    for kc in range(16):
        slot = kc % 2
        wav = wa[:, slot, :].k(slot)
        o.dma("pool", wav, dv(I["w_ada"][l, kc * 128:(kc + 1) * 128, :], "w_ada"), key=("wa", slot),
              max_dma_last_dim=2048)
        yield
        for cb in range(48):
            o.mm(accb[:, cb * 2:cb * 2 + 2], lhsT=wav[:, cb * 128:(cb + 1) * 128], rhs=scv[:, :, kc],
                 start=(kc == 0 and cb == 0), stop=(kc == 15))
        yield
    for cond in range(2):
        o.tt("dve", mod[:, l, cond, :], accb[:, 0:96].re("p (cb c) -> p c cb", c=2)[:, cond, :],
             badaT[:, l, :], ALU.add)
        o.ts("dve", t0[:, 0:16], mod[:, l, cond, 16:32], 1.0, 0.0, ALU.add, ALU.add)
        o.tt("dve", g1[:, l, cond, :], t0[:, 0:16], ngT[:, l * 16:(l + 1) * 16], ALU.mult)
        pt = get_bank()
        o.tr(pt[0:16, 0:128], mod[:, l, cond, 32:48], C["ident_f"][:, :])
        o.copy("dve", rowsT[0:16, :], pt[0:16, 0:128])
        o.dma("sp", dv(S["gscr"][l, cond], "gscr", l, cond), rowsT[0:16, :], key=("gs",))
    yield


def build_program(NS=8, debug=None):
    TS = NS * SEG
    NSEG = NS + 1
    nc = bass.Bass("TRN2", target_bir_lowering=False)

    def din(name, shape, dt=F32):
        return nc.dram_tensor(name, list(shape), dt, kind="ExternalInput").ap()

    def dout(name, shape, dt=F32):
        return nc.dram_tensor(name, list(shape), dt, kind="ExternalOutput").ap()

    def dint(name, shape, dt=F32):
        return nc.dram_tensor(name, list(shape), dt, kind="Internal").ap()

    I = {}
    I["xs"] = din("xs", [TS, D])
    I["xp"] = din("xp", [512, D])
    I["ck"] = din("ck", [DEPTH, 2, 512, 128])
    I["cv"] = din("cv", [DEPTH, 2, 512, 128])
    I["st"] = din("st", [DEPTH, 2, 4, 128, 128])
    I["cvec"] = din("cvec", [2, D])
    I["norm_g"] = din("norm_g", [DEPTH, D])
    I["w_ada"] = din("w_ada", [DEPTH, D, 3 * D])
    I["b_ada"] = din("b_ada", [DEPTH, 3 * D])
    I["w_in"] = din("w_in", [DEPTH, D, NCOL])
    I["qg"] = din("qg", [DEPTH, 128])
    I["kg"] = din("kg", [DEPTH, 128])
    I["sink"] = din("sink", [DEPTH, 8])
    I["lb"] = din("lb", [DEPTH, 2, 512])
    I["hg"] = din("hg", [DEPTH, 128])
    I["lng"] = din("lng", [DEPTH, 512])
    I["lnb"] = din("lnb", [DEPTH, 512])
    I["sgu_w"] = din("sgu_w", [DEPTH, 4, 128, 128])
    I["sgu_b"] = din("sgu_b", [DEPTH, 4, 128])
    I["w_out"] = din("w_out", [DEPTH, D, D])
    for cn, (shp, dt) in CONST_SPECS.items():
        I[cn] = din(cn, shp, dt)
    I["cosT"] = din("cosT", [128, TS])
    I["sinT"] = din("sinT", [128, TS])

    O = {}
    O["ys"] = dout("ys", [TS, D])
    O["yp"] = dout("yp", [512, D])
    O["nk"] = dout("nk", [2, DEPTH, 2, 256, 128])
    O["nv"] = dout("nv", [2, DEPTH, 2, 256, 128])
    O["ns"] = dout("ns", [2, DEPTH, 2, 4, 128, 128])

    S = {}
    S["wbf"] = dint("wbf", [DEPTH, D, NCOL], BF16)
    S["wobf"] = dint("wobf", [DEPTH, D, D], BF16)
    S["x1s"] = dint("x1s", [TS, D])
    S["x1p"] = dint("x1p", [512, D])
    S["sp_mixA"] = dint("sp_mixA", [NSEG, 128, 4, 8, 128], BF16)
    S["sp_mixC"] = dint("sp_mixC", [NSEG, 128, 4, 512], BF16)
    S["sp_o"] = dint("sp_o", [NSEG, 128, 4, 512])
    S["sp_qd"] = dint("sp_qd", [NSEG, 128, 4, 512], BF16)
    S["sp_sbg"] = dint("sp_sbg", [NSEG, 128, 4, 512], BF16)
    S["sp_U"] = dint("sp_U", [NSEG, 128, 512])
    S["gscr"] = dint("gscr", [DEPTH, 2, 16, 128])
    dbg = {}
    if debug:
        for name, (shp, dt) in debug.items():
            dbg[name] = dout("dbg_" + name, shp, dt)

    def dv(ap, *key):
        return V(ap, [("dram",) + tuple(key)])

    with ExitStack() as es:
        P = Prog(nc, es)
        LAST_PROG['P'] = P
        o = Ops(P)

        def sb(es_, name, shape, dt):
            return Buf(es_.enter_context(nc.sbuf_tensor(name, list(shape), dt)), name)

        psb = [Buf(es.enter_context(nc.psum_tensor("ps%d" % i, [128, 512], F32)), "ps%d" % i) for i in range(8)]

        def bank(i):
            return psb[i][:, :]

        acc_ring = Ring([0, 1])
        short_ring = Ring([2, 3])

        C = {}
        for cn, (shp, dt) in CONST_SPECS.items():
            C[cn] = sb(es, "c_" + cn, shp, dt)
        mod = sb(es, "mod", [128, DEPTH, 2, 48], F32)
        g1 = sb(es, "g1", [128, DEPTH, 2, 16], F32)
        small = sb(es, "small", [128, 8, DEPTH], F32)
        lbv = sb(es, "lbv", [128, DEPTH, 2, 4], F32)
        oml = sb(es, "oml", [128, DEPTH, 2, 4], F32)
        dseg = sb(es, "dseg", [128, NSEG, 4], F32)
        onecol = sb(es, "onecol", [128, 1], F32)
        sct_b = sb(es, "sct_b", [128, 32], BF16)
        badaT = sb(es, "badaT", [128, DEPTH, 48], F32)
        ngT = sb(es, "ngT", [128, DEPTH * 16], F32)
        ADA = dict(sct_b=sct_b, badaT=badaT, ngT=ngT, mod=mod, g1=g1)

        with ExitStack() as es0:
            if "noconst" not in OPTS:
                gC = P.group(("consts",))
                for cn in CONST_SPECS:
                    o.dma("sp", C[cn][:, :], dv(I[cn], cn), key=None, group=gC)
            if "nomemset" not in OPTS:
                o.memset("dve", onecol[:, :], 1.0)
            for l in range(1 if "nocast" not in OPTS else 0):
                g = P.group(("cast", l))
                for r in range(16):
                    o.dma("pool", V(S["wbf"][l, r * 128:(r + 1) * 128, :], [("dram", "wbf", l)]),
                          dv(I["w_in"][l, r * 128:(r + 1) * 128, :], "w_in"), key=None, group=g, carry=True,
                          max_dma_last_dim=2048)
                for r in range(16):
                    o.dma("pool", V(S["wobf"][l, r * 128:(r + 1) * 128, :], [("dram", "wobf", l)]),
                          dv(I["w_out"][l, r * 128:(r + 1) * 128, :], "w_out"), key=None, group=g, carry=True,
                          max_dma_last_dim=2048)

            rows = sb(es0, "rows", [128, 128], F32)
            rowsT = sb(es0, "rowsT", [128, 128], F32)
            wa = sb(es0, "wa", [128, 2, 3 * D], BF16)
            sct = sb(es0, "sct", [128, 32], F32)
            t0 = sb(es0, "t0", [128, 128], F32)
            t1 = sb(es0, "t1", [128, 128], F32)
            lbx = sb(es0, "lbx", [128, 16], F32)

            if "b0early" in OPTS:
                P.emit_block()
                return nc
            o.dma("sp", rows[0:32, :], dv(I["cvec"].rearrange("c (k f) -> (c k) f", f=128), "cvec"), key=("rows",))
            o.act(t0[0:32, :], rows[0:32, :], AF.Exp, scale=-1.0)
            o.ts("dve", t0[0:32, :], t0[0:32, :], 1.0, 0.0, ALU.add, ALU.add)
            o.recip(t0[0:32, :], t0[0:32, :])
            o.tt("dve", t1[0:32, :], rows[0:32, :], t0[0:32, :], ALU.mult)
            pt = bank(short_ring.get())
            o.tr(pt[:, 0:32], t1[0:32, :], C["ident_f"][0:32, 0:32])
            o.copy("dve", sct[:, :], pt[:, 0:32])
            o.copy("dve", sct_b[:, :], sct[:, :])
            o.dma("sp", rows[0:32, :], dv(I["norm_g"].rearrange("l (k f) -> (l k) f", f=128), "norm_g"), key=("rows",))
            pt = bank(short_ring.get())
            o.tr(pt[:, 0:32], rows[0:32, :], C["ident_f"][0:32, 0:32])
            o.copy("dve", ngT[:, :], pt[:, 0:32])
            for l in range(DEPTH):
                o.dma("sp", rows[0:48, :], dv(I["b_ada"][l].rearrange("(k f) -> k f", f=128), "b_ada"), key=("rows",))
                pt = bank(short_ring.get())
                o.tr(pt[:, 0:48], rows[0:48, :], C["ident_f"][0:48, 0:48])
                o.copy("dve", badaT[:, l, :], pt[:, 0:48])
            o.dma("sp", rows[0:2, :], dv(I["qg"], "qg"), key=("rows",))
            o.dma("sp", rows[2:4, :], dv(I["kg"], "kg"), key=("rows2",))
            o.dma("sp", rows[4:6, :], dv(I["hg"], "hg"), key=("rows3",))
            pt = bank(short_ring.get())
            o.tr(pt[:, 0:6], rows[0:6, :], C["ident_f"][0:6, 0:6])
            o.copy("dve", small[:, 0:3, :].re("p j l -> p (j l)"), pt[:, 0:6])
            o.dma("sp", rows[0:16, :], dv(I["lb"].rearrange("l r (h d) -> (l r h) d", d=128), "lb"), key=("rows",))
            pt = bank(short_ring.get())
            o.tr(pt[:, 0:16], rows[0:16, :], C["ident_f"][0:16, 0:16])
            o.copy("dve", lbx[:, :], pt[:, 0:16])
            x0 = lbx[:, 0:8]
            x1 = lbx[:, 8:16]
            o.tt("dve", t0[:, 0:8], x0, x1, ALU.max)
            o.tt("dve", t0[:, 8:16], x0, t0[:, 0:8], ALU.subtract)
            o.tt("dve", t0[:, 16:24], x1, t0[:, 0:8], ALU.subtract)
            o.act(t1[:, 8:24], t0[:, 8:24], AF.Exp)
            o.tt("dve", t1[:, 24:32], t1[:, 8:16], t1[:, 16:24], ALU.add)
            o.recip(t1[:, 32:40], t1[:, 24:32])
            o.memset("dve", lbv[:, 0, :, :], 0.0)
            o.tt("dve", lbv[:, 1, :, :].re("p r h -> p (r h)"), t1[:, 16:24], t1[:, 32:40], ALU.mult)
            o.ts("dve", oml[:, :, :, :].re("p l r h -> p (l r h)"), lbv[:, :, :, :].re("p l r h -> p (l r h)"),
                 -1.0, 1.0, ALU.mult, ALU.add)
            if "b0mid" in OPTS:
                P.emit_block()
                return nc
            for _ in adaln_gen(o, 0, I, S, C, ADA, wa, bank(short_ring.get()), lambda: bank(short_ring.get()), rowsT, t0):
                pass
            P.emit_block()
            if OPTS["stop"] == "b0":
                P.dead = True

        for l in range(DEPTH):
          try:
            xsrc_s = I["xs"] if l == 0 else S["x1s"]
            xsrc_p = I["xp"] if l == 0 else S["x1p"]
            xdst_s = S["x1s"] if l == 0 else O["ys"]
            xdst_p = S["x1p"] if l == 0 else O["yp"]
            xkey_in = ("x", l)
            xkey_out = ("x", l + 1)
            build_pass1(nc, P, o, l, NS, I, O, S, C, mod, g1, small, lbv, oml, dseg, onecol, psb,
                        xsrc_s, xsrc_p, xkey_in, dbg)
            build_pass2(nc, P, o, l, NS, I, O, S, C, small, dseg, psb, xsrc_s, xsrc_p, xdst_s, xdst_p,
                        xkey_in, xkey_out, dbg, ADA)
          except StopBuild:
            return nc
    return nc


LAST_PROG = {}


def build_pass1(nc, P, o, l, NS, I, O, S, C, mod, g1, small, lbv, oml, dseg, onecol, psb,
                xsrc_s, xsrc_p, xkey_in, dbg):
    NSEG = NS + 1

    def dv(ap, *key):
        return V(ap, [("dram",) + tuple(key)])

    def bank(i):
        return psb[i][:, :]

    acc_ring = Ring([0, 1])
    short_ring = Ring([2, 3])

    with ExitStack() as es1:
        def sb(name, shape, dt):
            return Buf(es1.enter_context(nc.sbuf_tensor(name + "_%d" % l, list(shape), dt)), name)

        xin = sb("xin", [128, 2, D], F32)
        hT = sb("hT", [128, 16, SEG], BF16)
        wt = sb("wt", [128, 3, 16, 256], BF16)
        NT = 14
        tmpb = sb("tmp", [128, NT, 512], F32)
        tmp_ring = Ring([tmpb[:, i, :].k(i) for i in range(NT)])
        stat = sb("stat", [128, 8, 8], F32)
        stat_ring = Ring([stat[:, i, :].k(i) for i in range(8)])
        qcur = sb("qcur", [128, 4, 8, 128], BF16)
        qlag = sb("qlag", [128, 8, 128], BF16)
        mixA = sb("mixA", [128, 4, 8, 128], BF16)
        mixAl = sb("mixAl", [128, 8, 128], BF16)
        kTb = sb("kTb", [128, 2, 6, 128], BF16)
        vtb = sb("vtb", [128, 6, 256], BF16)
        kctx = sb("kctx", [128, 2, 512], BF16)
        vctx = sb("vctx", [128, 2, 4, 128], BF16)
        sinkE = sb("sinkE", [128, 8], F32)
        pTb = sb("pT", [128, 4, 512], BF16)
        pT_ring = Ring([pTb[:, i, :].k(i) for i in range(4)])
        cs = sb("cs", [128, 512], F32)
        sn = sb("sn", [128, 512], F32)
        vn_tm = sb("vn_tm", [128, 4, 512], BF16)
        scg = sb("scg", [128, 4, 512], BF16)
        mixC = sb("mixC", [128, 4, 512], BF16)
        lngb = sb("lngb", [128, 512], F32)
        lnbb = sb("lnbb", [128, 512], F32)
        wsT = sb("wsT", [128, 4, 128], BF16)
        bsrow = sb("bsrow", [1, 512], F32)
        qs = sb("qs", [128, 512], F32)
        qe = sb("qe", [128, 2, 4, 512], BF16)
        keT = sb("keT", [128, 2, 4, 512], BF16)
        ketm = sb("ketm", [128, 2, 512], BF16)
        vtmb = sb("vtmb", [128, 4, 512], BF16)
        attm = sb("attm", [128, 2, 512], BF16)
        kem = sb("kem", [128, 2, 2, 512], BF16)
        Sst = sb("Sst", [128, 2, 512], F32)
        Stmp = sb("Stmp", [128, 2, 512], F32)
        Sbf = sb("Sbf", [128, 4, 512], BF16)
        Sbf_ring = Ring([Sbf[:, i, :].k(i) for i in range(4)])
        o_f = sb("o_f", [128, 4, 512], F32)
        dec = sb("dec", [128, 2, 4, 16], F32)

        ident_f = C["ident_f"][:, :]
        ident_b = C["ident_b"][:, :]
        ones_b = C["ones_b"][:, :]
        ones_f = C["ones_f"][:, :]
        qgcol = small[:, 0, l:l + 1]
        kgcol = small[:, 1, l:l + 1]

        o.dma("sp", sinkE[:, :], dv(I["sink"][l:l + 1, :].partition_broadcast(128), "sink"), key=("prep", 0))
        o.act(sinkE[:, :], sinkE[:, :], AF.Exp)
        o.dma("sp", lngb[:, :], dv(I["lng"][l:l + 1, :].partition_broadcast(128), "lng"), key=("prep", 1))
        o.dma("sp", lnbb[:, :], dv(I["lnb"][l:l + 1, :].partition_broadcast(128), "lnb"), key=("prep", 2))
        o.dma("sp", bsrow[:, :], dv(I["sgu_b"][l:l + 1].rearrange("o g p -> o (g p)"), "sgu_b"), key=("prep", 3))
        for g in range(4):
            t = tmp_ring.get()
            o.dma("sp", t[:, 0:128], dv(I["sgu_w"][l, g], "sgu_w"), key=("prep", 4 + (g % 2)))
            pt = bank(short_ring.get())
            o.tr(pt[:, 0:128], t[:, 0:128], ident_f)
            o.copy("dve", wsT[:, g, :].k(g), pt[:, 0:128])
        for hk in range(2):
            t = tmp_ring.get()
            o.dma("sp", t.re("p (b d) -> p b d", d=128), dv(I["ck"][l, hk].rearrange("(b p) d -> p b d", p=128), "ck"),
                  key=("prep", 6 + hk))
            pt = bank(short_ring.get())
            for b in range(4):
                o.tr(pt[:, b * 128:(b + 1) * 128], t[:, b * 128:(b + 1) * 128], ident_f)
            o.copy("act", kctx[:, hk, :].k(hk), pt)
            t = tmp_ring.get()
            o.dma("sp", t.re("p (b d) -> p b d", d=128), dv(I["cv"][l, hk].rearrange("(b p) d -> p b d", p=128), "cv"),
                  key=("prep", 8 + hk))
            o.copy("dve", vctx[:, hk, :, :].re("p b d -> p (b d)").k(hk), t)

        if l + 1 < DEPTH:
            g = P.group(("cast", l + 1))
            for r in range(16):
                o.dma("pool", V(S["wbf"][l + 1, r * 128:(r + 1) * 128, :], [("dram", "wbf", l + 1)]),
                      dv(I["w_in"][l + 1, r * 128:(r + 1) * 128, :], "w_in"), key=None, group=g, max_dma_last_dim=2048)
            for r in range(16):
                o.dma("pool", V(S["wobf"][l + 1, r * 128:(r + 1) * 128, :], [("dram", "wobf", l + 1)]),
                      dv(I["w_out"][l + 1, r * 128:(r + 1) * 128, :], "w_out"), key=None, group=g, max_dma_last_dim=2048)
        maybe_stop(P, "p1prep")
        segs = [dict(kind="p", idx=0, src=xsrc_p, row0=0, cond=1)]
        for s in range(NS):
            segs.append(dict(kind="s", idx=s + 1, src=xsrc_s, row0=s * SEG, cond=0, s=s))

        state = dict(xload_i=0, wload_i=0)
        queue = []

        def pump(n):
            n = n * OPTS.get("pump_mul", 1) + OPTS.get("pump_add", 0)
            while n > 0 and queue:
                try:
                    next(queue[0])
                    n -= 1
                except StopIteration:
                    queue.pop(0)

        def drain_all():
            while queue:
                try:
                    next(queue[0])
                except StopIteration:
                    queue.pop(0)

        def drain_task(task):
            while task in queue:
                try:
                    next(queue[0])
                except StopIteration:
                    queue.pop(0)

        allblocks = []
        for sg in segs:
            for b in range(4):
                allblocks.append((sg, b))

        def issue_xload(gi):
            if gi >= len(allblocks):
                return
            sg, b = allblocks[gi]
            slot = gi % 2
            r0 = sg["row0"] + b * 128
            o.dma("sp", xin[:, slot, :].k(slot), dv(sg["src"][r0:r0 + 128, :], *xkey_in), key=("xin", slot))

        issue_xload(0)
        issue_xload(1)

        def silu_to(out, Pin, blocked=False):
            e_ = tmp_ring.get()
            o.act(e_, Pin, AF.Exp, scale=-1.0)
            o.act(e_, e_, AF.Ln, bias=1.0)
            o.act(e_, e_, AF.Exp, scale=-1.0)
            if blocked:
                o.tt("dve", out, Pin.re("p (b t) -> p b t", b=4), e_.re("p (b t) -> p b t", b=4), ALU.mult)
            else:
                o.tt("dve", out, Pin, e_, ALU.mult)

        def rstd_from(ssum_v, n, out_small=None):
            lnv = tmp_ring.get()
            shape_cols = ssum_v.ap.shape[-1]
            o.act(lnv[:, 0:shape_cols], ssum_v, AF.Ln, scale=1.0 / n, bias=EPS)
            rs = tmp_ring.get()
            o.act(rs[:, 0:shape_cols], lnv[:, 0:shape_cols], AF.Exp, scale=-0.5)
            return rs

        mprev = C["mask_prev"][:, :]
        mnext = C["mask_next"][:, :]

        def kslot(hk, sl):
            return V(kTb.h[:, hk, sl, :], [("kTb", hk, "cur" if sl >= 2 else "lag")])

        def vslot(hk, sl):
            key = ("vtb", "cur", sl - 2) if sl >= 2 else ("vtb", "lag")
            return V(vtb.h[:, sl, hk * 128:(hk + 1) * 128], [key])

        def qview(b):
            return V(qcur.h[:, b, :, :], [("qcur", h_) for h_ in range(8)])

        def mview(b):
            return V(mixA.h[:, b, :, :], [("mixA", h_) for h_ in range(8)])

        qlag_v = V(qlag.h[:, :, :], [("qlag",)])
        mixAl_v = V(mixAl.h[:, :, :], [("mixAl",)])

        def attn_blocks(sg):
            blocks = []
            si = sg["idx"]
            if sg["kind"] == "p":
                for b in range(4):
                    seq = b // 2

                    def keys(hk, seq=seq):
                        return [(kslot(hk, sl), vslot(hk, sl), None) for sl in (2 + 2 * seq, 3 + 2 * seq)]
                    blocks.append(dict(qv=qview(b), mixv=mview(b), keys=keys))

                def after():
                    o.dma("pool", dv(S["sp_mixA"][si], "sp_mixA", si, "cur"),
                          V(mixA.h[:, :, :, :], [("mixA", h_) for h_ in range(8)]), key=("spa", 0))
                blocks[-1]["after"] = after
            else:
                s_ = sg["s"]
                last = (s_ == NS - 1)

                def ctxkeys(hk):
                    return [(kctx[:, hk, cb * 128:(cb + 1) * 128].k(hk), vctx[:, hk, cb, :].k(hk), None)
                            for cb in range(4)]
                if s_ > 0:
                    def keys_lag(hk):
                        return ctxkeys(hk) + [(kslot(hk, 0), vslot(hk, 0), mprev), (kslot(hk, 1), vslot(hk, 1), None),
                                              (kslot(hk, 2), vslot(hk, 2), mnext)]

                    def spill_lag():
                        o.dma("pool", dv(S["sp_mixA"][si - 1, :, 3], "sp_mixA", si - 1, "lag"), mixAl_v,
                              key=("spa", 1))
                    blocks.append(dict(qv=qlag_v, mixv=mixAl_v, keys=keys_lag, after=spill_lag))
                nb = 4 if last else 3
                for b in range(nb):
                    def keys(hk, b=b):
                        ks = ctxkeys(hk)
                        sl = 2 + b
                        if b > 0 or s_ > 0:
                            ks.append((kslot(hk, sl - 1), vslot(hk, sl - 1), mprev))
                        ks.append((kslot(hk, sl), vslot(hk, sl), None))
                        if b < 3:
                            ks.append((kslot(hk, sl + 1), vslot(hk, sl + 1), mnext))
                        return ks
                    blocks.append(dict(qv=qview(b), mixv=mview(b), keys=keys))

                def after():
                    o.dma("pool", dv(S["sp_mixA"][si, :, 0:nb], "sp_mixA", si, "cur"),
                          V(mixA.h[:, 0:nb, :, :], [("mixA", h_) for h_ in range(8)]), key=("spa", 0))
                    if not last:
                        o.copy("act", qlag_v, qview(3))
                        o.copy("dve", mixAl_v, mview(3))
                        o.copy("act", V(kTb.h[:, :, 0:2, :], [("kTb", 0, "lag"), ("kTb", 1, "lag")]),
                               V(kTb.h[:, :, 4:6, :], [("kTb", 0, "cur"), ("kTb", 1, "cur")]))
                        o.copy("dve", V(vtb.h[:, 0:2, :], [("vtb", "lag")]),
                               V(vtb.h[:, 4:6, :], [("vtb", "cur", 2), ("vtb", "cur", 3)]))
                blocks[-1]["after"] = after
            return blocks

        def attention_task(sg, blocks):
            for blk in blocks:
                for hk in range(2):
                    rhs_q = blk["qv"][:, 4 * hk:4 * hk + 4, :].re("p g t -> p (g t)")
                    oT = bank(4)
                    den = bank(5)
                    keys = blk["keys"](hk)
                    nk_ = len(keys)
                    LAG = 2
                    pts = {}
                    for i in range(nk_ + LAG):
                        if i < nk_:
                            kv, vv, mk = keys[i]
                            sc = bank(6 + (i % 2))
                            o.mm(sc, lhsT=kv, rhs=rhs_q)
                            pT = pT_ring.get()
                            o.act(pT, sc, AF.Exp, scale=ATT_SCALE)
                            if mk is not None:
                                o.tt("pool", pT, pT, mk, ALU.mult)
                            pts[i] = pT
                        j = i - LAG
                        if j >= 0:
                            kv, vv, mk = keys[j]
                            o.mm(oT, lhsT=vv, rhs=pts[j], start=(j == 0), stop=(j == nk_ - 1))
                            o.mm(den, lhsT=ones_b, rhs=pts[j], start=(j == 0), stop=(j == nk_ - 1))
                        yield
                    dsum = tmp_ring.get()
                    o.tt("dve", dsum.re("p (g t) -> p g t", g=4), den.re("p (g t) -> p g t", g=4),
                         sinkE[:, 4 * hk:4 * hk + 4].re("p (g o) -> p g o", o=1).bc([128, 4, 128]), ALU.add)
                    o.act(dsum, dsum, AF.Ln)
                    o.act(dsum, dsum, AF.Exp, scale=-1.0)
                    o1 = tmp_ring.get()
                    o.tt("dve", o1, oT, dsum, ALU.mult)
                    mv = blk["mixv"][:, 4 * hk:4 * hk + 4, :]
                    o.tt("dve", mv, o1.re("p (g t) -> p g t", g=4), mv, ALU.mult)
                    yield
                if blk.get("after") is not None:
                    blk["after"]()
                    yield

        def chain_task(sg):
            kind = sg["kind"]
            si = sg["idx"]
            hm = {0: C["hmask_f"][:, :], 1: C["hmask_b"][:, :]}
            slot_ctr = {"ketm": 0, "attm": 0, "kem": 0}
            ubank = Ring([6, 7])

            def block_setup(d, blk):
                tk = slice(blk * 128, (blk + 1) * 128)
                pt = bank(ubank.get())
                ptb = pt.bitcast(BF16)
                for h in range(4):
                    o.tr(ptb[:, h * 128:(h + 1) * 128], keT[:, d, h, tk].k(d, h), ident_b)
                kslot = ketm[:, d, :].k(d)
                o.copy("act", kslot, ptb[:, 0:512])
                pa = bank(ubank.get())
                for h in range(4):
                    o.mm(pa[:, h * 128:(h + 1) * 128], lhsT=keT[:, d, h, tk].k(d, h), rhs=qe[:, d, h, tk].k(d, h))
                aslot = attm[:, d, :].k(d)
                o.tt("dve", aslot, pa, hm[d], ALU.mult)
                c0 = 0 if d == 0 else 3
                o.ts("pool", kem[:, d, c0 % 2, :].k(d, c0 % 2), kslot, C["cmask"][:, c0:c0 + 1], 0.0, ALU.mult, ALU.add)
                return kslot, aslot

            def block_setup_b(d, blk, aslot):
                ob = bank(4 + d)
                for h in range(4):
                    o.mm(ob[:, h * 128:(h + 1) * 128], lhsT=vtmb[:, blk, h * 128:(h + 1) * 128].k(blk),
                         rhs=aslot[:, h * 128:(h + 1) * 128], start=(h == 0), stop=False)

            def chunk_a(d, blk, c, kslot, last):
                gch = blk * 4 + c
                Sd = Sst[:, d, :].k(d)
                St = Stmp[:, d, :].k(d)
                decb = dec[:, d, :, gch:gch + 1].k(d).bc([128, 4, 128])
                sbf = Sbf_ring.get()
                if d == 0:
                    o.copy("act", sbf, Sd)
                else:
                    o.tt("dve", St.re("p (h v) -> p h v", h=4), Sd.re("p (h v) -> p h v", h=4), decb, ALU.mult)
                    o.copy("act", sbf, St)
                km = kem[:, d, c % 2, :].k(d, c % 2)
                ub = bank(ubank.get())
                for h in range(4):
                    o.mm(ub[:, h * 128:(h + 1) * 128], lhsT=km[:, h * 128:(h + 1) * 128],
                         rhs=vtmb[:, blk, h * 128:(h + 1) * 128].k(blk))
                if d == 0:
                    o.tt("dve", St, Sd, ub, ALU.add)
                    o.tt("dve", Sd.re("p (h v) -> p h v", h=4), St.re("p (h v) -> p h v", h=4), decb, ALU.mult)
                else:
                    o.tt("dve", Sd, St, ub, ALU.add)
                if not last:
                    cn = c + 1 if d == 0 else c - 1
                    o.ts("pool", kem[:, d, cn % 2, :].k(d, cn % 2), kslot, C["cmask"][:, cn:cn + 1], 0.0,
                         ALU.mult, ALU.add)
                return sbf

            def chunk_b(d, blk, c, sbf, last):
                ob = bank(4 + d)
                for h in range(4):
                    t0_ = blk * 128 + c * 32
                    o.mm(ob[:, h * 128 + c * 32:h * 128 + c * 32 + 32], lhsT=sbf[:, h * 128:(h + 1) * 128],
                         rhs=qe[:, d, h, t0_:t0_ + 32].k(d, h), start=False, stop=last)

            def block_finish(d, blk):
                ob = bank(4 + d)
                tk = slice(blk * 128, (blk + 1) * 128)
                dst = o_f[:, :, tk].k(blk)
                first_dir = 0 if blk < 2 else 1
                if d == first_dir:
                    o.copy("act", dst, ob.re("p (h t) -> p h t", h=4))
                else:
                    o.tt("dve", dst, dst, ob.re("p (h t) -> p h t", h=4), ALU.add)

            def state_init(d, blk):
                Sd = Sst[:, d, :].k(d)
                if kind == "p":
                    if (d == 0 and blk in (0, 2)) or (d == 1 and blk in (3, 1)):
                        o.memset("dve", Sd, 0.0)
                else:
                    if d == 0 and blk == 0 and sg["s"] == 0:
                        o.dma("sp", Sd.re("p (h v) -> p h v", h=4),
                              dv(I["st"][l, 0].rearrange("h d v -> d h v"), "st"), key=("sinit",))
                    if d == 1 and blk == 3:
                        o.memset("dve", Sd, 0.0)

            def state_out(d, blk):
                Sd = Sst[:, d, :].k(d)
                if kind == "p":
                    if d == 0 and blk in (1, 3):
                        seq = blk // 2
                        o.dma("pool", dv(O["ns"][seq, l, 0].rearrange("h d v -> d h v"), "ns", seq, l, 0),
                              Sd.re("p (h v) -> p h v", h=4), key=("sout", 0))
                    if d == 1 and blk in (2, 0):
                        seq = blk // 2
                        o.dma("pool", dv(O["ns"][seq, l, 1].rearrange("h d v -> d h v"), "ns", seq, l, 1),
                              Sd.re("p (h v) -> p h v", h=4), key=("sout", 1))
                else:
                    if d == 1 and blk == 0:
                        o.dma("pool", dv(S["sp_U"][si], "sp_U", si), Sd, key=("sout", 1))

            fsteps = [(0, b, c) for b in range(4) for c in range(4)]
            bsteps = [(1, b, c) for b in (3, 2, 1, 0) for c in (3, 2, 1, 0)]
            kslots = {}
            pend = {}
            for i in range(16):
                for (d, b, c) in (fsteps[i], bsteps[i]):
                    first = (c == 0) if d == 0 else (c == 3)
                    last = (c == 3) if d == 0 else (c == 0)
                    if first:
                        state_init(d, b)
                        kslots[d], asl = block_setup(d, b)
                        yield
                        block_setup_b(d, b, asl)
                    sbf = chunk_a(d, b, c, kslots[d], last)
                    if d in pend:
                        pd = pend.pop(d)
                        chunk_b(*pd[0:5])
                        if pd[4]:
                            block_finish(pd[0], pd[1])
                    pend[d] = (d, b, c, sbf, last)
                    if last:
                        yield
                        pd = pend.pop(d)
                        chunk_b(*pd[0:5])
                        block_finish(d, b)
                        state_out(d, b)
                    yield
            o.dma("pool", dv(S["sp_o"][si], "sp_o", si),
                  V(o_f.h[:, :, :], [("o_f", b_) for b_ in range(4)]), key=("spo",))
            yield

        prev_attn = None
        prev_chain = None
        gblock = 0
        for sgi, sg in enumerate(segs):
            kind = sg["kind"]
            si = sg["idx"]
            cond = sg["cond"]
            rope = kind == "s"
            for b in range(4):
                gi = sgi * 4 + b
                slot = gi % 2
                xv = xin[:, slot, :].k(slot)
                st_ = stat_ring.get()
                junk = tmp_ring.get().bitcast(BF16)
                o.memset("dve", st_[:, 0:2], 0.0)
                o.act(junk, xv[:, 0:1024], AF.Square, accum=st_[:, 0:1])
                o.act(junk, xv[:, 1024:2048], AF.Square, accum=st_[:, 1:2])
                maybe_stop(P, "p1Na")
                o.tt("dve", st_[:, 2:3], st_[:, 0:1], st_[:, 1:2], ALU.add)
                o.act(st_[:, 3:4], st_[:, 2:3], AF.Ln, scale=1.0 / D, bias=EPS)
                o.act(st_[:, 4:5], st_[:, 3:4], AF.Exp, scale=-0.5)
                maybe_stop(P, "p1Nb")
                o.ts("dve", xv, xv, st_[:, 4:5], 0.0, ALU.mult, ALU.add)
                maybe_stop(P, "p1Nc")
                for k4 in range(4):
                    pt = bank(short_ring.get())
                    for j in range(4):
                        kc = k4 * 4 + j
                        o.tr(pt[:, j * 128:(j + 1) * 128], xv[:, kc * 128:(kc + 1) * 128], ident_f)
                    maybe_stop(P, "p1Nd")
                    for j in range(4):
                        kc = k4 * 4 + j
                        dst = hT[:, kc, b * 128:(b + 1) * 128].k(kc)
                        gcol = g1[:, l, cond, kc:kc + 1]
                        scol = mod[:, l, cond, kc:kc + 1]
                        if (k4 % 2 == 0 or "evac_act" in OPTS) and "evac_dve" not in OPTS:
                            o.act(dst, pt[:, j * 128:(j + 1) * 128], AF.Identity, scale=gcol, bias=scol)
                        else:
                            o.ts("dve", dst, pt[:, j * 128:(j + 1) * 128], gcol, scol, ALU.mult, ALU.add)
                issue_xload(gi + 2)
                maybe_stop(P, "p1Ne")
                pump(1)
            if "hT" in dbg and l == 0 and sgi == 0:
                o.dma("sp", V(dbg["hT"], [("dram", "dbg_hT")]), V(hT.h[:, :, :], [("hT", kc_) for kc_ in range(16)]),
                      key=("dbg",))
            maybe_stop(P, "p1N%d" % sgi)
            if rope:
                r0 = sg["row0"]
                o.dma("sp", cs[:, :], dv(I["cosT"][:, r0:r0 + SEG], "cosT"), key=("cs",))
                o.dma("sp", sn[:, :], dv(I["sinT"][:, r0:r0 + SEG], "sinT"), key=("sn",))

            def load_w(G):
                slot = state["wload_i"] % 3
                state["wload_i"] += 1
                wv = wt[:, slot, :, :].k(slot)
                src = S["wbf"][l].rearrange("(kc p) n -> p kc n", p=128)[:, :, G * 256:(G + 1) * 256]
                o.dma("sp", wv, V(src, [("dram", "wbf", l)]), key=("wt", slot))
                return wv

            def fm_block(wv, j):
                pb = bank(acc_ring.get())
                for kc in range(16):
                    o.mm(pb, lhsT=wv[:, kc, j * 128:(j + 1) * 128], rhs=hT[:, kc, :].k(kc),
                         start=(kc == 0), stop=(kc == 15))
                return pb

            def tm_block(wv, b, pb, c0):
                for kc in range(16):
                    o.mm(pb[:, c0:c0 + 256], lhsT=hT[:, kc, b * 128:(b + 1) * 128].k(kc), rhs=wv[:, kc, :],
                         start=(kc == 0), stop=(kc == 15))

            def qk_prep2(pbs, gcol, is_k, heads):
                n = len(pbs)
                sqb = [tmp_ring.get().bitcast(BF16)[:, 0:512] for _ in range(n)]
                qg_ = [tmp_ring.get() for _ in range(n)]
                for i in range(n):
                    o.act(sqb[i], pbs[i], AF.Square)
                    o.act(qg_[i], pbs[i], AF.Identity, scale=gcol)
                ss = [bank(short_ring.get()) for _ in range(n)]
                for i in range(n):
                    o.mm(ss[i], lhsT=ones_b, rhs=sqb[i])
                rs = [rstd_from(ss[i], 128.0) for i in range(n)]
                dests = []
                for i in range(n):
                    if is_k:
                        dests.append(kTb[:, heads[i], 2:6, :].k(heads[i], "cur"))
                    else:
                        dests.append(qcur[:, :, heads[i], :].k(heads[i]))
                if rope:
                    qnb = [tmp_ring.get().bitcast(BF16)[:, 0:512] for _ in range(n)]
                    for i in range(n):
                        o.tt("dve", qnb[i], qg_[i], rs[i], ALU.mult)
                    rot = [bank(short_ring.get()) for _ in range(n)]
                    t1_ = [tmp_ring.get() for _ in range(n)]
                    for i in range(n):
                        o.mm(rot[i], lhsT=C["rot_m"][:, :], rhs=qnb[i])
                        o.tt("pool", t1_[i], qnb[i], cs[:, :], ALU.mult)
                    for i in range(n):
                        t2_ = sqb[i].bitcast(F32) if False else qg_[i]
                        o.tt("dve", t2_, rot[i], sn[:, :], ALU.mult)
                        o.tt("dve", dests[i], t1_[i].re("p (b t) -> p b t", b=4), t2_.re("p (b t) -> p b t", b=4),
                             ALU.add)
                else:
                    for i in range(n):
                        head = heads[i]
                        if is_k:
                            qn = tmp_ring.get()
                            o.tt("dve", qn, qg_[i], rs[i], ALU.mult)
                            o.copy("act", dests[i], qn.re("p (b t) -> p b t", b=4))
                            for b in range(4):
                                pt = bank(short_ring.get())
                                o.tr(pt[:, 0:128], qn[:, b * 128:(b + 1) * 128], ident_f)
                                tk_ = tmp_ring.get()
                                o.copy("act", tk_[:, 0:128], pt[:, 0:128])
                                seq, tb = b // 2, b % 2
                                o.dma("pool", dv(O["nk"][seq, l, head, tb * 128:(tb + 1) * 128, :], "nk", seq, l, head, tb),
                                      tk_[:, 0:128], key=("nk", b % 2))
                        else:
                            o.tt("dve", dests[i], qg_[i].re("p (b t) -> p b t", b=4),
                                 rs[i].re("p (b t) -> p b t", b=4), ALU.mult)

            for G in range(26):
                if OPTS["stop"] == "p1s%dG%d" % (sgi, G) and l == 0:
                    drain_all()
                    maybe_stop(P, OPTS["stop"])
                if G == 0 and prev_attn is not None:
                    drain_task(prev_attn)
                    prev_attn = None
                if G == 16 and prev_chain is not None:
                    drain_task(prev_chain)
                    prev_chain = None
                wv = load_w(G)
                if G == 0:
                    pbs = [fm_block(wv, j) for j in range(2)]
                    qk_prep2(pbs, kgcol, True, [0, 1])
                    pump(4)
                elif G == 1:
                    for b in range(4):
                        pb = bank(acc_ring.get())
                        tm_block(wv, b, pb, 0)
                        o.copy("act", vtb[:, 2 + b, :].k("cur", b), pb[:, 0:256])
                        if kind == "p":
                            tv = tmp_ring.get()
                            o.copy("dve", tv[:, 0:256], pb[:, 0:256])
                            seq, tb = b // 2, b % 2
                            o.dma("pool", dv(O["nv"][seq, l, :, tb * 128:(tb + 1) * 128, :].rearrange("h t d -> t h d"),
                                             "nv", seq, l, tb),
                                  tv[:, 0:256].re("p (h d) -> p h d", h=2), key=("nv", b % 2))
                        pump(1)
                elif 2 <= G <= 5:
                    pbs = [fm_block(wv, j) for j in range(2)]
                    qk_prep2(pbs, qgcol, False, [(G - 2) * 2, (G - 2) * 2 + 1])
                    pump(4)
                elif 6 <= G <= 9:
                    for j in range(2):
                        head = (G - 6) * 2 + j
                        pb = fm_block(wv, j)
                        silu_to(mixA[:, :, head, :].k(head), pb, blocked=True)
                        pump(2)
                    if G == 9:
                        task = attention_task(sg, attn_blocks(sg))
                        queue.append(task)
                        prev_attn = task
                elif G in (10, 11):
                    if G == 10:
                        state["cv_w0"] = wv
                    else:
                        w0 = state["cv_w0"]
                        for b in range(4):
                            pb = bank(acc_ring.get())
                            tm_block(w0, b, pb, 0)
                            tm_block(wv, b, pb, 256)
                            st_ = stat_ring.get()
                            o.bn_stats(st_[:, 0:6], pb)
                            o.bn_aggr(st_[:, 6:8], st_[:, 0:6])
                            st2 = stat_ring.get()
                            o.act(st2[:, 0:1], st_[:, 7:8], AF.Ln, bias=EPS)
                            o.act(st2[:, 1:2], st2[:, 0:1], AF.Exp, scale=-0.5)
                            tn = tmp_ring.get()
                            o.ts("dve", tn, pb, st_[:, 6:7], st2[:, 1:2], ALU.subtract, ALU.mult)
                            o.tt("dve", tn, tn, lngb[:, :], ALU.mult)
                            o.tt("dve", vn_tm[:, b, :].k(b), tn, lnbb[:, :], ALU.add)
                            pump(2)
                elif G in (12, 13):
                    for j in range(2):
                        gg = (G - 12) * 2 + j
                        pb = fm_block(wv, j)
                        silu_to(scg[:, gg, :].k(gg), pb)
                        pump(2)
                elif G in (14, 15):
                    for j in range(2):
                        gg = (G - 14) * 2 + j
                        pb = fm_block(wv, j)
                        sps = bank(short_ring.get())
                        for b in range(4):
                            o.mm(sps[:, b * 128:(b + 1) * 128], lhsT=vn_tm[:, b, gg * 128:(gg + 1) * 128].k(b),
                                 rhs=wsT[:, gg, :].k(gg), start=True, stop=False)
                            o.mm(sps[:, b * 128:(b + 1) * 128], lhsT=ones_f[0:1, 0:128],
                                 rhs=bsrow[0:1, gg * 128:(gg + 1) * 128], start=False, stop=True)
                        s_sb = tmp_ring.get()
                        o.copy("act", s_sb, sps)
                        t_ = tmp_ring.get()
                        o.tt("dve", t_, pb, s_sb, ALU.mult)
                        o.tt("dve", mixC[:, gg, :].k(gg), t_, scg[:, gg, :].k(gg), ALU.mult)
                        pump(2)
                    if G == 15:
                        o.dma("pool", dv(S["sp_mixC"][si], "sp_mixC", si),
                              V(mixC.h[:, :, :], [("mixC", g_) for g_ in range(4)]), key=("spc",))
                elif G in (16, 17):
                    if G == 16:
                        state["bi_w0"] = wv
                    else:
                        w0 = state["bi_w0"]
                        for b in range(4):
                            pb = bank(acc_ring.get())
                            tm_block(w0, b, pb, 0)
                            tm_block(wv, b, pb, 256)
                            o.copy("act", vtmb[:, b, :].k(b), pb)
                            pump(2)
                else:
                    h = (G - 18) // 2
                    lbcol = lbv[:, l, :, h]
                    omlcol = oml[:, l, :, h]

                    def gates(pb, d):
                        e_ = tmp_ring.get()
                        o.act(e_, pb, AF.Exp, scale=-1.0)
                        o.act(e_, e_, AF.Ln, bias=1.0)
                        o.act(e_, e_, AF.Exp, scale=-1.0)
                        f_ = tmp_ring.get()
                        o.ts("dve", f_, e_, omlcol[:, d:d + 1], lbcol[:, d:d + 1], ALU.mult, ALU.add)
                        lf_ = tmp_ring.get()
                        o.act(lf_, f_, AF.Ln)
                        k_ = e_
                        o.ts("pool", k_, f_, -1.0, 1.0, ALU.mult, ALU.add)
                        return lf_, k_

                    if (G - 18) % 2 == 0:
                        pbq = fm_block(wv, 0)
                        silu_to(qs[:, :], pbq)
                        pump(1)
                        pbf = fm_block(wv, 1)
                        lf_, k_ = gates(pbf, 0)
                        c_ = tmp_ring.get()
                        o.scan(c_, C["scanmask"][:, :], lf_)
                        ec = tmp_ring.get()
                        o.act(ec, c_, AF.Exp)
                        en = lf_
                        o.act(en, c_, AF.Exp, scale=-1.0)
                        o.tt("dve", qe[:, 0, h, :].k(0, h), qs[:, :], ec, ALU.mult)
                        o.tt("dve", keT[:, 0, h, :].k(0, h), k_, en, ALU.mult)
                        o.copy("dve", dec[:, 0, h, :].k(0), ec.re("p (c j) -> p c j", j=32)[:, :, 31])
                        pump(2)
                    else:
                        pbb = fm_block(wv, 0)
                        lf_, k_ = gates(pbb, 1)
                        c_ = tmp_ring.get()
                        o.scan(c_, C["scanmask"][:, :], lf_)
                        o.act(dec[:, 1, h, :].k(1), c_.re("p (c j) -> p c j", j=32)[:, :, 31], AF.Exp)
                        cx = tmp_ring.get()
                        o.tt("pool", cx, c_, lf_, ALU.subtract)
                        ex = c_
                        o.act(ex, cx, AF.Exp)
                        enx = tmp_ring.get()
                        o.act(enx, cx, AF.Exp, scale=-1.0)
                        o.tt("dve", qe[:, 1, h, :].k(1, h), qs[:, :], enx, ALU.mult)
                        o.tt("dve", keT[:, 1, h, :].k(1, h), k_, ex, ALU.mult)
                        if kind == "s":
                            cseg = tmp_ring.get()
                            o.scan(cseg, onecol[:, 0:1].bc([128, 512]), lf_)
                            st_ = stat_ring.get()
                            o.copy("dve", st_[:, 0:1], cseg[:, 511:512])
                            o.act(dseg[:, si, h:h + 1].k(si, h), st_[:, 0:1], AF.Exp)
                            o.tt("dve", cx, cseg, lf_, ALU.subtract)
                            dg = enx
                            o.act(dg, cx, AF.Exp, scale=-1.0, bias=st_[:, 0:1])
                            qd = tmp_ring.get().bitcast(BF16)[:, 0:512]
                            o.tt("dve", qd, qs[:, :], dg, ALU.mult)
                            o.dma("pool", dv(S["sp_qd"][si, :, h, :], "sp_qd", si, h), qd, key=("spq", h % 2))
                        pump(1)
                        pbg = fm_block(wv, 1)
                        sbg = tmp_ring.get().bitcast(BF16)[:, 0:512]
                        silu_to(sbg, pbg)
                        o.dma("pool", dv(S["sp_sbg"][si, :, h, :], "sp_sbg", si, h), sbg, key=("sps", h % 2))
                        pump(2)
                    if G == 25:
                        task = chain_task(sg)
                        queue.append(task)
                        prev_chain = task
        drain_all()
        if l == 0:
            for nm_, src_ in (("mixA", "sp_mixA"), ("mixC", "sp_mixC"), ("spo", "sp_o")):
                if nm_ in dbg:
                    for si_ in range(NSEG):
                        o.dma("sp", V(dbg[nm_][si_], [("dram", "dbg", nm_, si_)]),
                              V(S[src_][si_], [("dram", src_, si_)] + [("dram", src_, si_, x_) for x_ in ("cur", "lag")]),
                              key=("dbg",))
        P.emit_block()
        if OPTS["stop"] == "p1":
            P.dead = True


def build_pass2(nc, P, o, l, NS, I, O, S, C, small, dseg, psb, xsrc_s, xsrc_p, xdst_s, xdst_p,
                xkey_in, xkey_out, dbg, ADA):
    NSEG = NS + 1

    def dv(ap, *key):
        return V(ap, [("dram",) + tuple(key)])

    def bank(i):
        return psb[i][:, :]

    acc_ring = Ring([0, 1, 2, 3])
    next_ada = (l + 1 < DEPTH)
    short_ring = Ring([4, 5, 6] if next_ada else [4, 5, 6, 7])

    with ExitStack() as es2:
        def sb(name, shape, dt):
            return Buf(es2.enter_context(nc.sbuf_tensor(name + "_%d" % l, list(shape), dt)), name)

        wo = sb("wo", [128, 16, D], BF16)
        gate = sb("gate", [128, 2, D], F32)
        sinb = sb("sinb", [128, NS, 512], BF16)
        sin_f = sb("sin_f", [128, 2, 512], F32)
        mA = sb("mA", [128, 2, 4, 8, 128], BF16)
        mC = sb("mC", [128, 2, 4, 512], BF16)
        mB = sb("mB", [128, 4, 512], BF16)
        ol = sb("ol", [128, 4, 512], F32)
        qd2 = sb("qd2", [128, 4, 512], BF16)
        sbg2 = sb("sbg2", [128, 4, 512], BF16)
        NXR = 2 if next_ada else 3
        xr = sb("xr", [128, NXR, D], F32)
        NT = 6 if next_ada else 10
        tmpb = sb("tmp2", [128, NT, 512], F32)
        ada_task = None
        if next_ada:
            wa2 = sb("wa2", [128, 2, 3 * D], BF16)
            rowsT2 = sb("rowsT2", [16, 128], F32)
            t02 = sb("t02", [128, 16], F32)
            ada_task = adaln_gen(o, l + 1, I, S, C, ADA, wa2, bank(7), lambda: bank(short_ring.get()), rowsT2, t02)
        tmp_ring = Ring([tmpb[:, i, :].k(i) for i in range(NT)])
        ones_b = C["ones_b"][:, :]
        hgcol = small[:, 2, l:l + 1]

        wsrc = S["wobf"][l].rearrange("(kc p) n -> p kc n", p=128)
        for i in range(4):
            o.dma("sp", wo[:, 4 * i:4 * i + 4, :].k(i), V(wsrc[:, 4 * i:4 * i + 4, :], [("dram", "wobf", l)]),
                  key=("wo", i))
        for cond in range(2):
            o.dma("sp", gate[:, cond, :].k(cond),
                  dv(S["gscr"][l, cond:cond + 1].rearrange("o k f -> o (k f)").partition_broadcast(128), "gscr"),
                  key=("gate", cond))

        cur = sin_f[:, 0, :].k(0)
        o.dma("sp", cur.re("p (h v) -> p h v", h=4), dv(I["st"][l, 1].rearrange("h d v -> d h v"), "st"),
              key=("sinit2",))
        o.copy("act", sinb[:, NS - 1, :].k(NS - 1), cur)
        for s_ in range(NS - 2, -1, -1):
            si_next = s_ + 2
            ut = tmp_ring.get()
            o.dma("sp", ut, dv(S["sp_U"][si_next], "sp_U"), key=("uld",))
            nxt = sin_f[:, (NS - 1 - s_) % 2, :].k((NS - 1 - s_) % 2)
            o.tt("dve", nxt.re("p (h v) -> p h v", h=4), cur.re("p (h v) -> p h v", h=4),
                 dseg[:, si_next, :].re("p (h o) -> p h o", o=1).bc([128, 4, 128]), ALU.mult)
            o.tt("dve", nxt, nxt, ut, ALU.add)
            o.copy("act", sinb[:, s_, :].k(s_), nxt)
            cur = nxt

        segs = [dict(kind="p", idx=0, src=xsrc_p, dst=xdst_p, row0=0, cond=1)]
        for s_ in range(NS):
            segs.append(dict(kind="s", idx=s_ + 1, src=xsrc_s, dst=xdst_s, row0=s_ * SEG, cond=0, s=s_))
        allblocks = [(sg, b) for sg in segs for b in range(4)]

        def issue_xload(gi):
            if gi >= len(allblocks):
                return
            sg, b = allblocks[gi]
            slot = gi % NXR
            r0 = sg["row0"] + b * 128
            o.dma("sp", xr[:, slot, :].k(slot), dv(sg["src"][r0:r0 + 128, :], *xkey_in), key=("xr", slot))

        def issue_segload(sgi):
            if sgi >= len(segs):
                return
            sg = segs[sgi]
            si = sg["idx"]
            slot = sgi % 2
            o.dma("sp", mA[:, slot, :, :, :].k(slot), dv(S["sp_mixA"][si], "sp_mixA"), key=("mA", slot))
            o.dma("sp", mC[:, slot, :, :].k(slot), dv(S["sp_mixC"][si], "sp_mixC"), key=("mC", slot))

        issue_segload(0)
        for gi_ in range(NXR - 1):
            issue_xload(gi_)
        for sgi, sg in enumerate(segs):
            si = sg["idx"]
            kind = sg["kind"]
            cond = sg["cond"]
            slot = sgi % 2
            def ada_step():
                nonlocal ada_task
                if ada_task is not None:
                    try:
                        next(ada_task)
                    except StopIteration:
                        ada_task = None
            ada_step()
            o.dma("sp", ol[:, :, :], dv(S["sp_o"][si], "sp_o"), key=("ol",))
            o.dma("sp", sbg2[:, :, :], dv(S["sp_sbg"][si], "sp_sbg"), key=("sbg2",))
            if kind == "s":
                o.dma("sp", qd2[:, :, :], dv(S["sp_qd"][si], "sp_qd"), key=("qd2",))
            issue_segload(sgi + 1)
            for h in range(4):
                if kind == "s":
                    pc = bank(short_ring.get())
                    o.mm(pc, lhsT=sinb[:, sg["s"], h * 128:(h + 1) * 128].k(sg["s"]), rhs=qd2[:, h, :])
                    o_ = tmp_ring.get()
                    o.tt("dve", o_, ol[:, h, :], pc, ALU.add)
                else:
                    o_ = ol[:, h, :]
                sqb = tmp_ring.get().bitcast(BF16)[:, 0:512]
                o.act(sqb, o_, AF.Square)
                ss = bank(short_ring.get())
                o.mm(ss, lhsT=ones_b, rhs=sqb)
                lnv = tmp_ring.get()
                o.act(lnv, ss, AF.Ln, scale=1.0 / 128.0, bias=EPS)
                rs = tmp_ring.get()
                o.act(rs, lnv, AF.Exp, scale=-0.5)
                on = tmp_ring.get()
                o.stt("dve", on, o_, hgcol, rs, ALU.mult, ALU.mult)
                o.tt("dve", mB[:, h, :].k(h), on, sbg2[:, h, :], ALU.mult)
            if "mB" in dbg and l == 0:
                o.dma("pool", V(dbg["mB"][si], [("dram", "dbg_mB", si)]), V(mB.h[:, :, :], [("mB", h_) for h_ in range(4)]),
                      key=("dbgmb",))
            for b in range(4):
                gi = sgi * 4 + b
                xslot = gi % NXR
                xv = xr[:, xslot, :].k(xslot)
                for n in range(4):
                    pb = bank(acc_ring.get())
                    for kc in range(16):
                        if kc < 8:
                            lhsT = mA[:, slot, b, kc, :].k(slot)
                        elif kc < 12:
                            lhsT = mB[:, kc - 8, b * 128:(b + 1) * 128].k(kc - 8)
                        else:
                            lhsT = mC[:, slot, kc - 12, b * 128:(b + 1) * 128].k(slot)
                        o.mm(pb, lhsT=lhsT, rhs=wo[:, kc, n * 512:(n + 1) * 512].k(kc // 4),
                             start=(kc == 0), stop=(kc == 15))
                    t_ = tmp_ring.get()
                    o.tt("dve", t_, pb, gate[:, cond, n * 512:(n + 1) * 512].k(cond), ALU.mult)
                    o.tt("dve", xv[:, n * 512:(n + 1) * 512], t_, xv[:, n * 512:(n + 1) * 512], ALU.add)
                r0 = sg["row0"] + b * 128
                o.dma("pool", dv(sg["dst"][r0:r0 + 128, :], *xkey_out), xv, key=("yout", xslot))
                issue_xload(gi + NXR - 1)
                if b == 1:
                    ada_step()
                    ada_step()
                if b == 3:
                    ada_step()
        if ada_task is not None:
            for _ in ada_task:
                pass
        P.emit_block()


_PROG_CACHE = {}


def _get_prog(NS):
    if NS not in _PROG_CACHE:
        _PROG_CACHE[NS] = build_program(NS)
    return _PROG_CACHE[NS]


def run_cores(per_core_inputs, NS):
    nc = _get_prog(NS)
    res = run_bass_kernel_spmd(nc, per_core_inputs, core_ids=list(range(len(per_core_inputs))))
    return res.results


def make_core_inputs(i, NS, x_prompt, x_sample, cache_k, cache_v, state_hgrn, c, c_ctx, shared):
    d = dict(shared)
    d["xs"] = np.ascontiguousarray(x_sample[i])
    d["xp"] = np.ascontiguousarray(x_prompt[2 * i:2 * i + 2].reshape(512, D))
    d["ck"] = np.ascontiguousarray(cache_k[i])
    d["cv"] = np.ascontiguousarray(cache_v[i])
    d["st"] = np.ascontiguousarray(state_hgrn[i])
    d["cvec"] = np.ascontiguousarray(np.stack([c[i], c_ctx], 0))
    return d


def make_shared(NS, norm_g, w_ada, b_ada, w_in, q_norm_g, k_norm_g, attn_sink, hgrn_lb, hgrn_norm_g,
                sgu_norm_g, sgu_norm_b, sgu_w, sgu_b, w_out):
    f = lambda a: np.ascontiguousarray(np.asarray(a, dtype=np.float32))
    sh = dict(norm_g=f(norm_g), w_ada=f(w_ada), b_ada=f(b_ada),
              w_in=np.ascontiguousarray(np.asarray(w_in, dtype=np.float32)[:, :, w_in_perm()]),
              qg=f(q_norm_g), kg=f(k_norm_g), sink=f(attn_sink), lb=f(hgrn_lb), hg=f(hgrn_norm_g),
              lng=f(sgu_norm_g), lnb=f(sgu_norm_b), sgu_w=f(sgu_w), sgu_b=f(sgu_b), w_out=f(w_out))
    sh.update(host_consts(NS * SEG))
    return sh


def kernel(x_prompt, x_sample, cache_k, cache_v, state_hgrn, c, c_ctx, norm_g, w_ada, b_ada, w_in,
           q_norm_g, k_norm_g, attn_sink, hgrn_lb, hgrn_norm_g, sgu_norm_g, sgu_norm_b, sgu_w, sgu_b, w_out):
    x_prompt = np.asarray(x_prompt, dtype=np.float32)
    x_sample = np.asarray(x_sample, dtype=np.float32)
    cache_k = np.asarray(cache_k, dtype=np.float32)
    cache_v = np.asarray(cache_v, dtype=np.float32)
    state_hgrn = np.asarray(state_hgrn, dtype=np.float32)
    c = np.asarray(c, dtype=np.float32)
    c_ctx = np.asarray(c_ctx, dtype=np.float32)
    NB = x_sample.shape[0]
    NS = x_sample.shape[1] // SEG
    shared = make_shared(NS, norm_g, w_ada, b_ada, w_in, q_norm_g, k_norm_g, attn_sink, hgrn_lb, hgrn_norm_g,
                         sgu_norm_g, sgu_norm_b, sgu_w, sgu_b, w_out)
    ins = [make_core_inputs(i, NS, x_prompt, x_sample, cache_k, cache_v, state_hgrn, c, c_ctx, shared)
           for i in range(NB)]
    res = run_cores(ins, NS)
    y_prompt = np.concatenate([r["yp"].reshape(2, 256, D) for r in res], 0).astype(np.float32)
    y_sample = np.stack([r["ys"] for r in res], 0).astype(np.float32)
    nk = np.concatenate([r["nk"] for r in res], 0).astype(np.float32)
    nv = np.concatenate([r["nv"] for r in res], 0).astype(np.float32)
    ns = np.concatenate([r["ns"] for r in res], 0).astype(np.float32)
    return (y_prompt, y_sample, nk, nv, ns)
```

```python
import numpy as np
import ml_dtypes
from contextlib import ExitStack
import concourse.bass as bass
import concourse.mybir as mybir
from concourse.bass_utils import run_bass_kernel_spmd

F32 = mybir.dt.float32
BF16 = mybir.dt.bfloat16
AF = mybir.ActivationFunctionType
ALU = mybir.AluOpType

D = 2048
DEPTH = 2
NCOL = 6656
EPS = 1e-6
SEG = 512
ATT_SCALE = 128 ** -0.5


class V:
    __slots__ = ("ap", "keys")

    def __init__(self, ap, keys):
        self.ap = ap
        self.keys = tuple(keys)

    def __getitem__(self, idx):
        return V(self.ap[idx], self.keys)

    def re(self, s, **kw):
        return V(self.ap.rearrange(s, **kw), self.keys)

    def bc(self, shape):
        return V(self.ap.to_broadcast(list(shape)), self.keys)

    def bitcast(self, dt):
        return V(self.ap.bitcast(dt), self.keys)

    def k(self, *suffix):
        return V(self.ap, [self.keys[0] + tuple(suffix)])


class Buf:
    def __init__(self, handle, name):
        self.h = handle
        self.name = name

    def __getitem__(self, idx):
        return V(self.h[idx], [(self.name,)])


class Op:
    __slots__ = ("eng", "fn", "deps", "is_dma", "dkey", "grp", "needs_sig", "sig", "waits", "idx", "cnt")


class Group:
    __slots__ = ("key", "final", "last_op")


COMPUTE = ("pe", "act", "dve", "pool")


class Prog:
    def __init__(self, nc, es, max_dma_sems=80):
        self.nc = nc
        self.es = es
        self.eng_sem = {e: es.enter_context(nc.semaphore("sem_" + e)) for e in COMPUTE}
        self.sig_count = {e: 0 for e in COMPUTE}
        self.dma_sem = {}
        self.dma_count = {}
        self.dma_prev_group = {}
        self.max_dma_sems = max_dma_sems
        self.carry = {}
        self.dead = False
        self.reset_block()

    def reset_block(self):
        self.ops = []
        self.last_w = dict(self.carry)
        self.readers = {}
        self.block_dma_ops = []

    def _sem_for(self, key):
        if key not in self.dma_sem:
            assert len(self.dma_sem) < self.max_dma_sems, "too many DMA semaphores"
            self.dma_sem[key] = self.es.enter_context(self.nc.semaphore("dsem%d" % len(self.dma_sem)))
            self.dma_count[key] = 0
        return self.dma_sem[key]

    def group(self, key):
        g = Group()
        g.key = key
        g.final = None
        g.last_op = None
        self._sem_for(key)
        return g

    def add(self, eng, fn, reads=(), writes=(), dma_key=None, group=None, carry=False):
        if self.dead:
            return None
        op = Op()
        op.eng = eng
        op.fn = fn
        op.is_dma = dma_key is not None or group is not None
        op.needs_sig = False
        op.sig = None
        op.grp = None
        op.dkey = None
        op.idx = len(self.ops)
        deps = []
        for r in reads:
            w = self.last_w.get(r)
            if w is not None:
                deps.append((w, "raw"))
            if r[0].startswith("ps") and len(r) == 1:
                for rd in self.readers.get(r, {}).values():
                    if rd.eng != eng:
                        deps.append((rd, "raw"))
        for wkey in writes:
            w = self.last_w.get(wkey)
            if w is not None:
                deps.append((w, "waw"))
            for rd in self.readers.get(wkey, {}).values():
                deps.append((rd, "war"))
        if op.is_dma:
            if group is None:
                group = self.group(dma_key)
            key = group.key
            op.dkey = key
            op.grp = group
            prev = self.dma_prev_group.get(key)
            if prev is not None and prev is not group and prev.last_op is not None:
                deps.append((prev.last_op, "ser"))
            deps = [(d_, k_) for (d_, k_) in deps if not (d_.is_dma and d_.grp is group)]
            self.dma_count[key] += 16
            group.final = self.dma_count[key]
            group.last_op = op
            self.dma_prev_group[key] = group
            self.block_dma_ops.append(op)
        op.deps = deps
        for r in reads:
            self.readers.setdefault(r, {})[("dma", op.idx) if op.is_dma else op.eng] = op
        for wkey in writes:
            self.last_w[wkey] = op
            self.readers[wkey] = {}
            if carry:
                self.carry[wkey] = op
            elif wkey in self.carry:
                del self.carry[wkey]
        self.ops.append(op)
        return op

    def _plan(self):
        for op in self.ops:
            eff = []
            for d, kind in op.deps:
                if d is op:
                    continue
                if (not d.is_dma) and (not op.is_dma) and d.eng == op.eng:
                    if op.eng == "pe":
                        continue
                    if kind != "raw":
                        continue
                eff.append(d)
                if not d.is_dma:
                    d.needs_sig = True
            op.deps = eff
        for op in self.ops:
            if op.is_dma:
                continue
            if op.needs_sig and op.sig is None:
                self.sig_count[op.eng] += 1
                op.sig = self.sig_count[op.eng]
        eng_vc = {}
        op_vc = {}
        for op in self.ops:
            vc = eng_vc.setdefault(op.eng, {})
            waits = []
            for d in op.deps:
                if d.is_dma:
                    sem, val = ("d", d.dkey), d.grp.final
                else:
                    sem, val = ("e", d.eng), d.sig
                if vc.get(sem, 0) >= val:
                    continue
                waits.append((sem, val))
                dvc = op_vc.get(id(d))
                if dvc is not None:
                    for s2, v2 in dvc.items():
                        if vc.get(s2, 0) < v2:
                            vc[s2] = v2
                if vc.get(sem, 0) < val:
                    vc[sem] = val
            ww = {}
            for s, v in waits:
                if ww.get(s, 0) < v:
                    ww[s] = v
            op.waits = list(ww.items())
            if op.is_dma:
                nv = dict(vc)
                nv[("d", op.dkey)] = op.grp.final
                op_vc[id(op)] = nv
            elif op.needs_sig:
                nv = dict(vc)
                nv[("e", op.eng)] = op.sig
                op_vc[id(op)] = nv

    def _check_deadlock(self, by_eng):
        if not hasattr(self, "sim"):
            self.sim = {}
        sim = self.sim
        ptr = {e: 0 for e in by_eng}
        progress = True
        while progress:
            progress = False
            for e, lst in by_eng.items():
                while ptr[e] < len(lst):
                    op = lst[ptr[e]]
                    if any(sim.get(sem, 0) < val for sem, val in op.waits):
                        break
                    if op.is_dma:
                        k_ = ("d", op.dkey)
                        sim[k_] = sim.get(k_, 0) + 16
                    elif op.needs_sig:
                        k_ = ("e", op.eng)
                        sim[k_] = sim.get(k_, 0) + 1
                        assert sim[k_] == op.sig, (k_, sim[k_], op.sig)
                    ptr[e] += 1
                    progress = True
        stuck = {e: ptr[e] for e in by_eng if ptr[e] < len(by_eng[e])}
        if stuck:
            msg = []
            for e, i in stuck.items():
                op = by_eng[e][i]
                msg.append("%s stuck at op %d/%d waits=%s have=%s" % (
                    e, i, len(by_eng[e]), op.waits, [(sm, sim.get(sm, 0)) for sm, _ in op.waits]))
            raise RuntimeError("DEADLOCK: " + " | ".join(msg))

    def emit_block(self, final_wait=True, skip_final=()):
        if self.dead:
            return
        if final_wait:
            seen = {}
            for op in self.block_dma_ops:
                if op.dkey in skip_final:
                    continue
                seen[(op.eng, op.dkey)] = op
            for eng in sorted(set(k_[0] for k_ in seen)):
                fin = self.add(eng, lambda e: None, reads=(), writes=())
                fin.deps = [(o_, "raw") for (e_, _), o_ in seen.items() if e_ == eng]
        self._plan()
        by_eng = {}
        for op in self.ops:
            by_eng.setdefault(op.eng, []).append(op)
        self._check_deadlock(by_eng)
        nc = self.nc
        prog = self

        def run(eng_name, e):
            for op in by_eng.get(eng_name, ()):
                for sem, val in op.waits:
                    if sem[0] == "d":
                        e.wait_ge(prog.dma_sem[sem[1]], val)
                    else:
                        e.wait_ge(prog.eng_sem[sem[1]], val)
                inst = op.fn(e)
                if op.is_dma:
                    inst.then_inc(prog.dma_sem[op.dkey], 16)
                elif op.needs_sig:
                    assert inst is not None
                    inst.then_inc(prog.eng_sem[op.eng], 1)

        with nc.Block() as block:
            @block.tensor
            def _(e):
                run("pe", e)

            @block.scalar
            def _(e):
                run("act", e)

            @block.vector
            def _(e):
                run("dve", e)

            @block.gpsimd
            def _(e):
                run("pool", e)

            @block.sync
            def _(e):
                run("sp", e)
        self.reset_block()


def _k(*vs):
    out = []
    for v in vs:
        if isinstance(v, V):
            out.extend(v.keys)
    return out


def _ap(x):
    return x.ap if isinstance(x, V) else x


class Ops:
    def __init__(self, P):
        self.P = P

    def mm(self, out, lhsT, rhs, start=True, stop=True):
        self.P.add("pe", lambda e: e.matmul(out.ap, lhsT=lhsT.ap, rhs=rhs.ap, start=start, stop=stop),
                   reads=_k(lhsT, rhs), writes=_k(out))

    def tr(self, out, in_, ident):
        self.P.add("pe", lambda e: e.transpose(out.ap, in_.ap, ident.ap), reads=_k(in_, ident), writes=_k(out))

    def act(self, out, in_, func, bias=None, scale=None, accum=None):
        kw = {}
        if bias is not None:
            kw["bias"] = _ap(bias)
        if scale is not None:
            kw["scale"] = _ap(scale)
        if accum is not None:
            kw["accum_out"] = accum.ap
        self.P.add("act", lambda e: e.activation(out.ap, in_.ap, func, **kw),
                   reads=_k(in_, bias, scale), writes=_k(out, accum))

    def tt(self, eng, out, in0, in1, op):
        self.P.add(eng, lambda e: e.tensor_tensor(out.ap, in0.ap, in1.ap, op), reads=_k(in0, in1), writes=_k(out))

    def ts(self, eng, out, in0, s1, s2, op0, op1):
        self.P.add(eng, lambda e: e.tensor_scalar(out.ap, in0.ap, _ap(s1), _ap(s2), op0, op1),
                   reads=_k(in0, s1, s2), writes=_k(out))

    def stt(self, eng, out, in0, scalar, in1, op0, op1):
        self.P.add(eng, lambda e: e.scalar_tensor_tensor(out.ap, in0.ap, _ap(scalar), in1.ap, op0, op1),
                   reads=_k(in0, scalar, in1), writes=_k(out))

    def copy(self, eng, out, in_):
        if eng == "act":
            self.P.add("act", lambda e: e.activation(out.ap, in_.ap, AF.Copy), reads=_k(in_), writes=_k(out))
        else:
            self.P.add(eng, lambda e: e.tensor_copy(out.ap, in_.ap), reads=_k(in_), writes=_k(out))

    def recip(self, out, in_):
        self.P.add("dve", lambda e: e.reciprocal(out.ap, in_.ap), reads=_k(in_), writes=_k(out))

    def scan(self, out, d0, d1, init=0.0):
        self.P.add("dve", lambda e: e.tensor_tensor_scan(out.ap, d0.ap, d1.ap, init, ALU.mult, ALU.add),
                   reads=_k(d0, d1), writes=_k(out))

    def memset(self, eng, out, val):
        self.P.add(eng, lambda e: e.memset(out.ap, val), reads=(), writes=_k(out))

    def bn_stats(self, out, in_):
        self.P.add("dve", lambda e: e.bn_stats(out.ap, in_.ap), reads=_k(in_), writes=_k(out))

    def bn_aggr(self, out, in_):
        self.P.add("dve", lambda e: e.bn_aggr(out.ap, in_.ap), reads=_k(in_), writes=_k(out))

    def dma(self, q, out, in_, key, group=None, carry=False, **kw):
        self.P.add(q, lambda e: e.dma_start(out=out.ap, in_=in_.ap, **kw), reads=_k(in_), writes=_k(out),
                   dma_key=key, group=group, carry=carry)


class StopBuild(Exception):
    pass


OPTS = {"stop": None}


def maybe_stop(P, tag):
    if OPTS["stop"] == tag and not P.dead:
        P.emit_block()
        P.dead = True


class Ring:
    def __init__(self, views):
        self.views = views
        self.i = 0

    def get(self):
        v = self.views[self.i % len(self.views)]
        self.i += 1
        return v


def w_in_perm():
    aq, ak, av, ag = 0, 1024, 1280, 1536
    bq, bff, bfb, bi, bg = 2560, 3072, 3584, 4096, 4608
    cu, cv, cg = 5120, 5632, 6144
    r = lambda s, n: list(range(s, s + n))
    cols = []
    cols += r(ak, 256) + r(av, 256)
    cols += r(aq, 1024) + r(ag, 1024)
    cols += r(cv, 512) + r(cg, 512) + r(cu, 512)
    cols += r(bi, 512)
    for h in range(4):
        cols += r(bq + 128 * h, 128) + r(bff + 128 * h, 128) + r(bfb + 128 * h, 128) + r(bg + 128 * h, 128)
    assert len(cols) == NCOL and len(set(cols)) == NCOL
    return np.array(cols, dtype=np.int64)


def host_consts(TS):
    c = {}
    c["ident_f"] = np.eye(128, dtype=np.float32)
    c["ident_b"] = np.eye(128, dtype=np.float32).astype(ml_dtypes.bfloat16)
    rm = np.zeros((128, 128), np.float32)
    for base in (0, 64):
        for f in range(32):
            rm[base + 32 + f, base + f] = -1.0
            rm[base + f, base + 32 + f] = 1.0
    c["rot_m"] = rm.astype(ml_dtypes.bfloat16)
    c["ones_b"] = np.ones((128, 128), np.float32).astype(ml_dtypes.bfloat16)
    c["ones_f"] = np.ones((128, 128), np.float32)
    kk = np.arange(128)[:, None]
    qq = np.arange(128)[None, :]
    mp = (kk >= qq).astype(np.float32)
    mn = (kk <= qq).astype(np.float32)
    c["mask_prev"] = np.tile(mp, (1, 4)).astype(ml_dtypes.bfloat16)
    c["mask_next"] = np.tile(mn, (1, 4)).astype(ml_dtypes.bfloat16)
    same = (kk // 32) == (qq // 32)
    hf = (same & (kk <= qq)).astype(np.float32)
    hb = (same & (kk >= qq)).astype(np.float32)
    c["hmask_f"] = np.tile(hf, (1, 4)).astype(np.float32)
    c["hmask_b"] = np.tile(hb, (1, 4)).astype(np.float32)
    sm = np.ones((128, 512), np.float32)
    sm[:, ::32] = 0.0
    c["scanmask"] = sm
    cm = np.zeros((128, 4), np.float32)
    for ch in range(4):
        cm[32 * ch:32 * ch + 32, ch] = 1.0
    c["cmask"] = cm
    t = np.arange(TS)
    row = (t // 64).astype(np.float32)
    col = (t % 64).astype(np.float32)
    half = 64
    freq = (10000.0 ** (-np.arange(0, half, 2, dtype=np.float32) / half)).astype(np.float32)
    ang_r = row[None, :] * freq[:, None]
    ang_c = col[None, :] * freq[:, None]
    cosT = np.concatenate([np.cos(ang_r), np.cos(ang_r), np.cos(ang_c), np.cos(ang_c)], 0).astype(np.float32)
    sinT = np.concatenate([np.sin(ang_r), np.sin(ang_r), np.sin(ang_c), np.sin(ang_c)], 0).astype(np.float32)
    c["cosT"] = cosT
    c["sinT"] = sinT
    return c


CONST_SPECS = {
    "ident_f": ([128, 128], F32), "ident_b": ([128, 128], BF16), "rot_m": ([128, 128], BF16),
    "ones_b": ([128, 128], BF16), "ones_f": ([128, 128], F32),
    "mask_prev": ([128, 512], BF16), "mask_next": ([128, 512], BF16),
    "hmask_f": ([128, 512], F32), "hmask_b": ([128, 512], F32),
    "scanmask": ([128, 512], F32), "cmask": ([128, 4], F32),
}


def adaln_gen(o, l, I, S, C, ADA, wa, accb, get_bank, rowsT, t0):
    mod, g1, badaT, ngT = ADA["mod"], ADA["g1"], ADA["badaT"], ADA["ngT"]
    scv = ADA["sct_b"][:, :].re("p (c k) -> p c k", c=2)

    def dv(ap, *key):
        return V(ap, [("dram",) + tuple(key)])
    for kc in range(16):
        slot = kc % 2
        wav = wa[:, slot, :].k(slot)
        o.dma("pool", wav, dv(I["w_ada"][l, kc * 128:(kc + 1) * 128, :], "w_ada"), key=("wa", slot),
              max_dma_last_dim=2048)
        yield
        for cb in range(48):
            o.mm(accb[:, cb * 2:cb * 2 + 2], lhsT=wav[:, cb * 128:(cb + 1) * 128], rhs=scv[:, :, kc],
                 start=(kc == 0 and cb == 0), stop=(kc == 15))
        yield
    for cond in range(2):
        o.tt("dve", mod[:, l, cond, :], accb[:, 0:96].re("p (cb c) -> p c cb", c=2)[:, cond, :],
             badaT[:, l, :], ALU.add)
        o.ts("dve", t0[:, 0:16], mod[:, l, cond, 16:32], 1.0, 0.0, ALU.add, ALU.add)
        o.tt("dve", g1[:, l, cond, :], t0[:, 0:16], ngT[:, l * 16:(l + 1) * 16], ALU.mult)
        pt = get_bank()
        o.tr(pt[0:16, 0:128], mod[:, l, cond, 32:48], C["ident_f"][:, :])
        o.copy("dve", rowsT[0:16, :], pt[0:16, 0:128])
        o.dma("sp", dv(S["gscr"][l, cond], "gscr", l, cond), rowsT[0:16, :], key=("gs",))
    yield


def build_program(NS=8, debug=None):
    TS = NS * SEG
    NSEG = NS + 1
    nc = bass.Bass("TRN2", target_bir_lowering=False)

    def din(name, shape, dt=F32):
        return nc.dram_tensor(name, list(shape), dt, kind="ExternalInput").ap()

    def dout(name, shape, dt=F32):
        return nc.dram_tensor(name, list(shape), dt, kind="ExternalOutput").ap()

    def dint(name, shape, dt=F32):
        return nc.dram_tensor(name, list(shape), dt, kind="Internal").ap()

    I = {}
    I["xs"] = din("xs", [TS, D])
    I["xp"] = din("xp", [512, D])
    I["ck"] = din("ck", [DEPTH, 2, 512, 128])
    I["cv"] = din("cv", [DEPTH, 2, 512, 128])
    I["st"] = din("st", [DEPTH, 2, 4, 128, 128])
    I["cvec"] = din("cvec", [2, D])
    I["norm_g"] = din("norm_g", [DEPTH, D])
    I["w_ada"] = din("w_ada", [DEPTH, D, 3 * D])
    I["b_ada"] = din("b_ada", [DEPTH, 3 * D])
    I["w_in"] = din("w_in", [DEPTH, D, NCOL])
    I["qg"] = din("qg", [DEPTH, 128])
    I["kg"] = din("kg", [DEPTH, 128])
    I["sink"] = din("sink", [DEPTH, 8])
    I["lb"] = din("lb", [DEPTH, 2, 512])
    I["hg"] = din("hg", [DEPTH, 128])
    I["lng"] = din("lng", [DEPTH, 512])
    I["lnb"] = din("lnb", [DEPTH, 512])
    I["sgu_w"] = din("sgu_w", [DEPTH, 4, 128, 128])
    I["sgu_b"] = din("sgu_b", [DEPTH, 4, 128])
    I["w_out"] = din("w_out", [DEPTH, D, D])
    for cn, (shp, dt) in CONST_SPECS.items():
        I[cn] = din(cn, shp, dt)
    I["cosT"] = din("cosT", [128, TS])
    I["sinT"] = din("sinT", [128, TS])

    O = {}
    O["ys"] = dout("ys", [TS, D])
    O["yp"] = dout("yp", [512, D])
    O["nk"] = dout("nk", [2, DEPTH, 2, 256, 128])
    O["nv"] = dout("nv", [2, DEPTH, 2, 256, 128])
    O["ns"] = dout("ns", [2, DEPTH, 2, 4, 128, 128])

    S = {}
    S["wbf"] = dint("wbf", [DEPTH, D, NCOL], BF16)
    S["wobf"] = dint("wobf", [DEPTH, D, D], BF16)
    S["x1s"] = dint("x1s", [TS, D])
    S["x1p"] = dint("x1p", [512, D])
    S["sp_mixA"] = dint("sp_mixA", [NSEG, 128, 4, 8, 128], BF16)
    S["sp_mixC"] = dint("sp_mixC", [NSEG, 128, 4, 512], BF16)
    S["sp_o"] = dint("sp_o", [NSEG, 128, 4, 512])
    S["sp_qd"] = dint("sp_qd", [NSEG, 128, 4, 512], BF16)
    S["sp_sbg"] = dint("sp_sbg", [NSEG, 128, 4, 512], BF16)
    S["sp_U"] = dint("sp_U", [NSEG, 128, 512])
    S["gscr"] = dint("gscr", [DEPTH, 2, 16, 128])
    dbg = {}
    if debug:
        for name, (shp, dt) in debug.items():
            dbg[name] = dout("dbg_" + name, shp, dt)

    def dv(ap, *key):
        return V(ap, [("dram",) + tuple(key)])

    with ExitStack() as es:
        P = Prog(nc, es)
        LAST_PROG['P'] = P
        o = Ops(P)

        def sb(es_, name, shape, dt):
            return Buf(es_.enter_context(nc.sbuf_tensor(name, list(shape), dt)), name)

        psb = [Buf(es.enter_context(nc.psum_tensor("ps%d" % i, [128, 512], F32)), "ps%d" % i) for i in range(8)]

        def bank(i):
            return psb[i][:, :]

        acc_ring = Ring([0, 1])
        short_ring = Ring([2, 3])

        C = {}
        for cn, (shp, dt) in CONST_SPECS.items():
            C[cn] = sb(es, "c_" + cn, shp, dt)
        mod = sb(es, "mod", [128, DEPTH, 2, 48], F32)
        g1 = sb(es, "g1", [128, DEPTH, 2, 16], F32)
        small = sb(es, "small", [128, 8, DEPTH], F32)
        lbv = sb(es, "lbv", [128, DEPTH, 2, 4], F32)
        oml = sb(es, "oml", [128, DEPTH, 2, 4], F32)
        dseg = sb(es, "dseg", [128, NSEG, 4], F32)
        onecol = sb(es, "onecol", [128, 1], F32)
        sct_b = sb(es, "sct_b", [128, 32], BF16)
        badaT = sb(es, "badaT", [128, DEPTH, 48], F32)
        ngT = sb(es, "ngT", [128, DEPTH * 16], F32)
        ADA = dict(sct_b=sct_b, badaT=badaT, ngT=ngT, mod=mod, g1=g1)

        with ExitStack() as es0:
            if "noconst" not in OPTS:
                gC = P.group(("consts",))
                for cn in CONST_SPECS:
                    o.dma("sp", C[cn][:, :], dv(I[cn], cn), key=None, group=gC)
            if "nomemset" not in OPTS:
                o.memset("dve", onecol[:, :], 1.0)
            for l in range(1 if "nocast" not in OPTS else 0):
                g = P.group(("cast", l))
                for r in range(16):
                    o.dma("pool", V(S["wbf"][l, r * 128:(r + 1) * 128, :], [("dram", "wbf", l)]),
                          dv(I["w_in"][l, r * 128:(r + 1) * 128, :], "w_in"), key=None, group=g, carry=True,
                          max_dma_last_dim=2048)
                for r in range(16):
                    o.dma("pool", V(S["wobf"][l, r * 128:(r + 1) * 128, :], [("dram", "wobf", l)]),
                          dv(I["w_out"][l, r * 128:(r + 1) * 128, :], "w_out"), key=None, group=g, carry=True,
                          max_dma_last_dim=2048)

            rows = sb(es0, "rows", [128, 128], F32)
            rowsT = sb(es0, "rowsT", [128, 128], F32)
            wa = sb(es0, "wa", [128, 2, 3 * D], BF16)
            sct = sb(es0, "sct", [128, 32], F32)
            t0 = sb(es0, "t0", [128, 128], F32)
            t1 = sb(es0, "t1", [128, 128], F32)
            lbx = sb(es0, "lbx", [128, 16], F32)

            if "b0early" in OPTS:
                P.emit_block()
                return nc
            o.dma("sp", rows[0:32, :], dv(I["cvec"].rearrange("c (k f) -> (c k) f", f=128), "cvec"), key=("rows",))
            o.act(t0[0:32, :], rows[0:32, :], AF.Exp, scale=-1.0)
            o.ts("dve", t0[0:32, :], t0[0:32, :], 1.0, 0.0, ALU.add, ALU.add)
            o.recip(t0[0:32, :], t0[0:32, :])
            o.tt("dve", t1[0:32, :], rows[0:32, :], t0[0:32, :], ALU.mult)
            pt = bank(short_ring.get())
            o.tr(pt[:, 0:32], t1[0:32, :], C["ident_f"][0:32, 0:32])
            o.copy("dve", sct[:, :], pt[:, 0:32])
            o.copy("dve", sct_b[:, :], sct[:, :])
            o.dma("sp", rows[0:32, :], dv(I["norm_g"].rearrange("l (k f) -> (l k) f", f=128), "norm_g"), key=("rows",))
            pt = bank(short_ring.get())
            o.tr(pt[:, 0:32], rows[0:32, :], C["ident_f"][0:32, 0:32])
            o.copy("dve", ngT[:, :], pt[:, 0:32])
            for l in range(DEPTH):
                o.dma("sp", rows[0:48, :], dv(I["b_ada"][l].rearrange("(k f) -> k f", f=128), "b_ada"), key=("rows",))
                pt = bank(short_ring.get())
                o.tr(pt[:, 0:48], rows[0:48, :], C["ident_f"][0:48, 0:48])
                o.copy("dve", badaT[:, l, :], pt[:, 0:48])
            o.dma("sp", rows[0:2, :], dv(I["qg"], "qg"), key=("rows",))
            o.dma("sp", rows[2:4, :], dv(I["kg"], "kg"), key=("rows2",))
            o.dma("sp", rows[4:6, :], dv(I["hg"], "hg"), key=("rows3",))
            pt = bank(short_ring.get())
            o.tr(pt[:, 0:6], rows[0:6, :], C["ident_f"][0:6, 0:6])
            o.copy("dve", small[:, 0:3, :].re("p j l -> p (j l)"), pt[:, 0:6])
            o.dma("sp", rows[0:16, :], dv(I["lb"].rearrange("l r (h d) -> (l r h) d", d=128), "lb"), key=("rows",))
            pt = bank(short_ring.get())
            o.tr(pt[:, 0:16], rows[0:16, :], C["ident_f"][0:16, 0:16])
            o.copy("dve", lbx[:, :], pt[:, 0:16])
            x0 = lbx[:, 0:8]
            x1 = lbx[:, 8:16]
            o.tt("dve", t0[:, 0:8], x0, x1, ALU.max)
            o.tt("dve", t0[:, 8:16], x0, t0[:, 0:8], ALU.subtract)
            o.tt("dve", t0[:, 16:24], x1, t0[:, 0:8], ALU.subtract)
            o.act(t1[:, 8:24], t0[:, 8:24], AF.Exp)
            o.tt("dve", t1[:, 24:32], t1[:, 8:16], t1[:, 16:24], ALU.add)
            o.recip(t1[:, 32:40], t1[:, 24:32])
            o.memset("dve", lbv[:, 0, :, :], 0.0)
            o.tt("dve", lbv[:, 1, :, :].re("p r h -> p (r h)"), t1[:, 16:24], t1[:, 32:40], ALU.mult)
            o.ts("dve", oml[:, :, :, :].re("p l r h -> p (l r h)"), lbv[:, :, :, :].re("p l r h -> p (l r h)"),
                 -1.0, 1.0, ALU.mult, ALU.add)
            if "b0mid" in OPTS:
                P.emit_block()
                return nc
            for _ in adaln_gen(o, 0, I, S, C, ADA, wa, bank(short_ring.get()), lambda: bank(short_ring.get()), rowsT, t0):
                pass
            P.emit_block()
            if OPTS["stop"] == "b0":
                P.dead = True

        for l in range(DEPTH):
          try:
            xsrc_s = I["xs"] if l == 0 else S["x1s"]
            xsrc_p = I["xp"] if l == 0 else S["x1p"]
            xdst_s = S["x1s"] if l == 0 else O["ys"]
            xdst_p = S["x1p"] if l == 0 else O["yp"]
            xkey_in = ("x", l)
            xkey_out = ("x", l + 1)
            build_pass1(nc, P, o, l, NS, I, O, S, C, mod, g1, small, lbv, oml, dseg, onecol, psb,
                        xsrc_s, xsrc_p, xkey_in, dbg)
            build_pass2(nc, P, o, l, NS, I, O, S, C, small, dseg, psb, xsrc_s, xsrc_p, xdst_s, xdst_p,
                        xkey_in, xkey_out, dbg, ADA)
          except StopBuild:
            return nc
    return nc


LAST_PROG = {}


def build_pass1(nc, P, o, l, NS, I, O, S, C, mod, g1, small, lbv, oml, dseg, onecol, psb,
                xsrc_s, xsrc_p, xkey_in, dbg):
    NSEG = NS + 1

    def dv(ap, *key):
        return V(ap, [("dram",) + tuple(key)])

    def bank(i):
        return psb[i][:, :]

    acc_ring = Ring([0, 1])
    short_ring = Ring([2, 3])

    with ExitStack() as es1:
        def sb(name, shape, dt):
            return Buf(es1.enter_context(nc.sbuf_tensor(name + "_%d" % l, list(shape), dt)), name)

        xin = sb("xin", [128, 2, D], F32)
        hT = sb("hT", [128, 16, SEG], BF16)
        wt = sb("wt", [128, 3, 16, 256], BF16)
        NT = 14
        tmpb = sb("tmp", [128, NT, 512], F32)
        tmp_ring = Ring([tmpb[:, i, :].k(i) for i in range(NT)])
        stat = sb("stat", [128, 8, 8], F32)
        stat_ring = Ring([stat[:, i, :].k(i) for i in range(8)])
        qcur = sb("qcur", [128, 4, 8, 128], BF16)
        qlag = sb("qlag", [128, 8, 128], BF16)
        mixA = sb("mixA", [128, 4, 8, 128], BF16)
        mixAl = sb("mixAl", [128, 8, 128], BF16)
        kTb = sb("kTb", [128, 2, 6, 128], BF16)
        vtb = sb("vtb", [128, 6, 256], BF16)
        kctx = sb("kctx", [128, 2, 512], BF16)
        vctx = sb("vctx", [128, 2, 4, 128], BF16)
        sinkE = sb("sinkE", [128, 8], F32)
        pTb = sb("pT", [128, 4, 512], BF16)
        pT_ring = Ring([pTb[:, i, :].k(i) for i in range(4)])
        cs = sb("cs", [128, 512], F32)
        sn = sb("sn", [128, 512], F32)
        vn_tm = sb("vn_tm", [128, 4, 512], BF16)
        scg = sb("scg", [128, 4, 512], BF16)
        mixC = sb("mixC", [128, 4, 512], BF16)
        lngb = sb("lngb", [128, 512], F32)
        lnbb = sb("lnbb", [128, 512], F32)
        wsT = sb("wsT", [128, 4, 128], BF16)
        bsrow = sb("bsrow", [1, 512], F32)
        qs = sb("qs", [128, 512], F32)
        qe = sb("qe", [128, 2, 4, 512], BF16)
        keT = sb("keT", [128, 2, 4, 512], BF16)
        ketm = sb("ketm", [128, 2, 512], BF16)
        vtmb = sb("vtmb", [128, 4, 512], BF16)
        attm = sb("attm", [128, 2, 512], BF16)
        kem = sb("kem", [128, 2, 2, 512], BF16)
        Sst = sb("Sst", [128, 2, 512], F32)
        Stmp = sb("Stmp", [128, 2, 512], F32)
        Sbf = sb("Sbf", [128, 6, 512], BF16)
        Sbf_rings = [Ring([Sbf[:, 3 * d_ + i, :].k(3 * d_ + i) for i in range(3)]) for d_ in range(2)]
        o_f = sb("o_f", [128, 4, 512], F32)
        dec = sb("dec", [128, 2, 4, 16], F32)

        ident_f = C["ident_f"][:, :]
        ident_b = C["ident_b"][:, :]
        ones_b = C["ones_b"][:, :]
        ones_f = C["ones_f"][:, :]
        qgcol = small[:, 0, l:l + 1]
        kgcol = small[:, 1, l:l + 1]

        o.dma("sp", sinkE[:, :], dv(I["sink"][l:l + 1, :].partition_broadcast(128), "sink"), key=("prep", 0))
        o.act(sinkE[:, :], sinkE[:, :], AF.Exp)
        o.dma("sp", lngb[:, :], dv(I["lng"][l:l + 1, :].partition_broadcast(128), "lng"), key=("prep", 1))
        o.dma("sp", lnbb[:, :], dv(I["lnb"][l:l + 1, :].partition_broadcast(128), "lnb"), key=("prep", 2))
        o.dma("sp", bsrow[:, :], dv(I["sgu_b"][l:l + 1].rearrange("o g p -> o (g p)"), "sgu_b"), key=("prep", 3))
        for g in range(4):
            t = tmp_ring.get()
            o.dma("sp", t[:, 0:128], dv(I["sgu_w"][l, g], "sgu_w"), key=("prep", 4 + (g % 2)))
            pt = bank(short_ring.get())
            o.tr(pt[:, 0:128], t[:, 0:128], ident_f)
            o.copy("dve", wsT[:, g, :].k(g), pt[:, 0:128])
        for hk in range(2):
            t = tmp_ring.get()
            o.dma("sp", t.re("p (b d) -> p b d", d=128), dv(I["ck"][l, hk].rearrange("(b p) d -> p b d", p=128), "ck"),
                  key=("prep", 6 + hk))
            pt = bank(short_ring.get())
            for b in range(4):
                o.tr(pt[:, b * 128:(b + 1) * 128], t[:, b * 128:(b + 1) * 128], ident_f)
            o.copy("act", kctx[:, hk, :].k(hk), pt)
            t = tmp_ring.get()
            o.dma("sp", t.re("p (b d) -> p b d", d=128), dv(I["cv"][l, hk].rearrange("(b p) d -> p b d", p=128), "cv"),
                  key=("prep", 8 + hk))
            o.copy("dve", vctx[:, hk, :, :].re("p b d -> p (b d)").k(hk), t)

        if l + 1 < DEPTH:
            g = P.group(("cast", l + 1))
            for r in range(16):
                o.dma("pool", V(S["wbf"][l + 1, r * 128:(r + 1) * 128, :], [("dram", "wbf", l + 1)]),
                      dv(I["w_in"][l + 1, r * 128:(r + 1) * 128, :], "w_in"), key=None, group=g, max_dma_last_dim=2048)
            for r in range(16):
                o.dma("pool", V(S["wobf"][l + 1, r * 128:(r + 1) * 128, :], [("dram", "wobf", l + 1)]),
                      dv(I["w_out"][l + 1, r * 128:(r + 1) * 128, :], "w_out"), key=None, group=g, max_dma_last_dim=2048)
        maybe_stop(P, "p1prep")
        segs = []
        for s in range(NS):
            segs.append(dict(kind="s", idx=s + 1, src=xsrc_s, row0=s * SEG, cond=0, s=s))
        segs.append(dict(kind="p", idx=0, src=xsrc_p, row0=0, cond=1))

        state = dict(xload_i=0, wload_i=0)
        queue = []

        def pump(n):
            n = n * OPTS.get("pump_mul", 1) + OPTS.get("pump_add", 0)
            while n > 0 and queue:
                try:
                    next(queue[0])
                    n -= 1
                except StopIteration:
                    queue.pop(0)

        def drain_all():
            while queue:
                try:
                    next(queue[0])
                except StopIteration:
                    queue.pop(0)

        def drain_task(task):
            while task in queue:
                try:
                    next(queue[0])
                except StopIteration:
                    queue.pop(0)

        allblocks = []
        for sg in segs:
            for b in range(4):
                allblocks.append((sg, b))

        def issue_xload(gi):
            if gi >= len(allblocks):
                return
            sg, b = allblocks[gi]
            slot = gi % 2
            r0 = sg["row0"] + b * 128
            o.dma("sp", xin[:, slot, :].k(slot), dv(sg["src"][r0:r0 + 128, :], *xkey_in), key=("xin", slot))

        issue_xload(0)
        issue_xload(1)

        def silu_to(out, Pin, blocked=False):
            e_ = tmp_ring.get()
            o.act(e_, Pin, AF.Exp, scale=-1.0)
            o.act(e_, e_, AF.Ln, bias=1.0)
            o.act(e_, e_, AF.Exp, scale=-1.0)
            if blocked:
                o.tt("dve", out, Pin.re("p (b t) -> p b t", b=4), e_.re("p (b t) -> p b t", b=4), ALU.mult)
            else:
                o.tt("dve", out, Pin, e_, ALU.mult)

        def rstd_from(ssum_v, n, out_small=None):
            lnv = tmp_ring.get()
            shape_cols = ssum_v.ap.shape[-1]
            o.act(lnv[:, 0:shape_cols], ssum_v, AF.Ln, scale=1.0 / n, bias=EPS)
            rs = tmp_ring.get()
            o.act(rs[:, 0:shape_cols], lnv[:, 0:shape_cols], AF.Exp, scale=-0.5)
            return rs

        mprev = C["mask_prev"][:, :]
        mnext = C["mask_next"][:, :]

        def kslot(hk, sl):
            return V(kTb.h[:, hk, sl, :], [("kTb", hk, "cur" if sl >= 2 else "lag")])

        def vslot(hk, sl):
            key = ("vtb", "cur", sl - 2) if sl >= 2 else ("vtb", "lag")
            return V(vtb.h[:, sl, hk * 128:(hk + 1) * 128], [key])

        def qview(b):
            return V(qcur.h[:, b, :, :], [("qcur", h_) for h_ in range(8)])

        def mview(b):
            return V(mixA.h[:, b, :, :], [("mixA", h_) for h_ in range(8)])

        qlag_v = V(qlag.h[:, :, :], [("qlag",)])
        mixAl_v = V(mixAl.h[:, :, :], [("mixAl",)])

        def attn_blocks(sg):
            blocks = []
            si = sg["idx"]
            if sg["kind"] == "p":
                for b in range(4):
                    seq = b // 2

                    def keys(hk, seq=seq):
                        return [(kslot(hk, sl), vslot(hk, sl), None) for sl in (2 + 2 * seq, 3 + 2 * seq)]
                    blocks.append(dict(qv=qview(b), mixv=mview(b), keys=keys))

                def after():
                    o.dma("pool", dv(S["sp_mixA"][si], "sp_mixA", si, "cur"),
                          V(mixA.h[:, :, :, :], [("mixA", h_) for h_ in range(8)]), key=("spa", 0))
                blocks[-1]["after"] = after
            else:
                s_ = sg["s"]
                last = (s_ == NS - 1)

                def ctxkeys(hk):
                    return [(kctx[:, hk, cb * 128:(cb + 1) * 128].k(hk), vctx[:, hk, cb, :].k(hk), None)
                            for cb in range(4)]
                if s_ > 0:
                    def keys_lag(hk):
                        return ctxkeys(hk) + [(kslot(hk, 0), vslot(hk, 0), mprev), (kslot(hk, 1), vslot(hk, 1), None),
                                              (kslot(hk, 2), vslot(hk, 2), mnext)]

                    def spill_lag():
                        o.dma("pool", dv(S["sp_mixA"][si - 1, :, 3], "sp_mixA", si - 1, "lag"), mixAl_v,
                              key=("spa", 1))
                    blocks.append(dict(qv=qlag_v, mixv=mixAl_v, keys=keys_lag, after=spill_lag))
                nb = 4 if last else 3
                for b in range(nb):
                    def keys(hk, b=b):
                        ks = ctxkeys(hk)
                        sl = 2 + b
                        if b > 0 or s_ > 0:
                            ks.append((kslot(hk, sl - 1), vslot(hk, sl - 1), mprev))
                        ks.append((kslot(hk, sl), vslot(hk, sl), None))
                        if b < 3:
                            ks.append((kslot(hk, sl + 1), vslot(hk, sl + 1), mnext))
                        return ks
                    blocks.append(dict(qv=qview(b), mixv=mview(b), keys=keys))

                def after():
                    o.dma("pool", dv(S["sp_mixA"][si, :, 0:nb], "sp_mixA", si, "cur"),
                          V(mixA.h[:, 0:nb, :, :], [("mixA", h_) for h_ in range(8)]), key=("spa", 0))
                    if not last:
                        o.copy("act", qlag_v, qview(3))
                        o.copy("dve", mixAl_v, mview(3))
                        o.copy("act", V(kTb.h[:, :, 0:2, :], [("kTb", 0, "lag"), ("kTb", 1, "lag")]),
                               V(kTb.h[:, :, 4:6, :], [("kTb", 0, "cur"), ("kTb", 1, "cur")]))
                        o.copy("dve", V(vtb.h[:, 0:2, :], [("vtb", "lag")]),
                               V(vtb.h[:, 4:6, :], [("vtb", "cur", 2), ("vtb", "cur", 3)]))
                blocks[-1]["after"] = after
            return blocks

        def attention_task(sg, blocks):
            for blk in blocks:
                for hk in range(2):
                    rhs_q = blk["qv"][:, 4 * hk:4 * hk + 4, :].re("p g t -> p (g t)")
                    oT = bank(4)
                    den = bank(5)
                    keys = blk["keys"](hk)
                    nk_ = len(keys)
                    LAG = 2
                    pts = {}
                    for i in range(nk_ + LAG):
                        if i < nk_:
                            kv, vv, mk = keys[i]
                            sc = bank(6 + (i % 2))
                            o.mm(sc, lhsT=kv, rhs=rhs_q)
                            pT = pT_ring.get()
                            o.act(pT, sc, AF.Exp, scale=ATT_SCALE)
                            if mk is not None:
                                o.tt("pool", pT, pT, mk, ALU.mult)
                            pts[i] = pT
                        j = i - LAG
                        if j >= 0:
                            kv, vv, mk = keys[j]
                            o.mm(oT, lhsT=vv, rhs=pts[j], start=(j == 0), stop=(j == nk_ - 1))
                            o.mm(den, lhsT=ones_b, rhs=pts[j], start=(j == 0), stop=(j == nk_ - 1))
                        yield
                    dsum = tmp_ring.get()
                    o.tt("dve", dsum.re("p (g t) -> p g t", g=4), den.re("p (g t) -> p g t", g=4),
                         sinkE[:, 4 * hk:4 * hk + 4].re("p (g o) -> p g o", o=1).bc([128, 4, 128]), ALU.add)
                    o.act(dsum, dsum, AF.Ln)
                    o.act(dsum, dsum, AF.Exp, scale=-1.0)
                    o1 = tmp_ring.get()
                    o.tt("dve", o1, oT, dsum, ALU.mult)
                    mv = blk["mixv"][:, 4 * hk:4 * hk + 4, :]
                    o.tt("dve", mv, o1.re("p (g t) -> p g t", g=4), mv, ALU.mult)
                    yield
                if blk.get("after") is not None:
                    blk["after"]()
                    yield

        def chain_task(sg):
            kind = sg["kind"]
            si = sg["idx"]
            hm = {0: C["hmask_f"][:, :], 1: C["hmask_b"][:, :]}
            slot_ctr = {"ketm": 0, "attm": 0, "kem": 0}
            ubank = Ring([6, 7])

            def block_setup(d, blk):
                tk = slice(blk * 128, (blk + 1) * 128)
                pt = bank(ubank.get())
                ptb = pt.bitcast(BF16)
                for h in range(4):
                    o.tr(ptb[:, h * 128:(h + 1) * 128], keT[:, d, h, tk].k(d, h), ident_b)
                kslot = ketm[:, d, :].k(d)
                o.copy("act", kslot, ptb[:, 0:512])
                pa = bank(ubank.get())
                for h in range(4):
                    o.mm(pa[:, h * 128:(h + 1) * 128], lhsT=keT[:, d, h, tk].k(d, h), rhs=qe[:, d, h, tk].k(d, h))
                aslot = attm[:, d, :].k(d)
                o.tt("dve", aslot, pa, hm[d], ALU.mult)
                c0 = 0 if d == 0 else 3
                o.ts("pool", kem[:, d, c0 % 2, :].k(d, c0 % 2), kslot, C["cmask"][:, c0:c0 + 1], 0.0, ALU.mult, ALU.add)
                return kslot, aslot

            def block_setup_b(d, blk, aslot):
                ob = bank(4 + d)
                for h in range(4):
                    o.mm(ob[:, h * 128:(h + 1) * 128], lhsT=vtmb[:, blk, h * 128:(h + 1) * 128].k(blk),
                         rhs=aslot[:, h * 128:(h + 1) * 128], start=(h == 0), stop=False)

            def chunk_a(d, blk, c, kslot, last):
                gch = blk * 4 + c
                Sd = Sst[:, d, :].k(d)
                St = Stmp[:, d, :].k(d)
                decb = dec[:, d, :, gch:gch + 1].k(d).bc([128, 4, 128])
                sbf = Sbf_rings[d].get()
                if d == 0:
                    o.copy("act", sbf, Sd)
                else:
                    o.tt("dve", St.re("p (h v) -> p h v", h=4), Sd.re("p (h v) -> p h v", h=4), decb, ALU.mult)
                    o.copy("act", sbf, St)
                km = kem[:, d, c % 2, :].k(d, c % 2)
                ub = bank(ubank.get())
                for h in range(4):
                    o.mm(ub[:, h * 128:(h + 1) * 128], lhsT=km[:, h * 128:(h + 1) * 128],
                         rhs=vtmb[:, blk, h * 128:(h + 1) * 128].k(blk))
                if d == 0:
                    o.tt("dve", St, Sd, ub, ALU.add)
                    o.tt("dve", Sd.re("p (h v) -> p h v", h=4), St.re("p (h v) -> p h v", h=4), decb, ALU.mult)
                else:
                    o.tt("dve", Sd, St, ub, ALU.add)
                if not last:
                    cn = c + 1 if d == 0 else c - 1
                    o.ts("pool", kem[:, d, cn % 2, :].k(d, cn % 2), kslot, C["cmask"][:, cn:cn + 1], 0.0,
                         ALU.mult, ALU.add)
                return sbf

            def chunk_b(d, blk, c, sbf, last):
                ob = bank(4 + d)
                for h in range(4):
                    t0_ = blk * 128 + c * 32
                    o.mm(ob[:, h * 128 + c * 32:h * 128 + c * 32 + 32], lhsT=sbf[:, h * 128:(h + 1) * 128],
                         rhs=qe[:, d, h, t0_:t0_ + 32].k(d, h), start=False, stop=last)

            def block_finish(d, blk):
                ob = bank(4 + d)
                tk = slice(blk * 128, (blk + 1) * 128)
                dst = o_f[:, :, tk].k(blk)
                first_dir = 0 if blk < 2 else 1
                if d == first_dir:
                    o.copy("act", dst, ob.re("p (h t) -> p h t", h=4))
                else:
                    o.tt("dve", dst, dst, ob.re("p (h t) -> p h t", h=4), ALU.add)

            def state_init(d, blk):
                Sd = Sst[:, d, :].k(d)
                if kind == "p":
                    if (d == 0 and blk in (0, 2)) or (d == 1 and blk in (3, 1)):
                        o.memset("dve", Sd, 0.0)
                else:
                    if d == 0 and blk == 0 and sg["s"] == 0:
                        o.dma("sp", Sd.re("p (h v) -> p h v", h=4),
                              dv(I["st"][l, 0].rearrange("h d v -> d h v"), "st"), key=("sinit",))
                    if d == 1 and blk == 3:
                        o.memset("dve", Sd, 0.0)

            def state_out(d, blk):
                Sd = Sst[:, d, :].k(d)
                if kind == "p":
                    if d == 0 and blk in (1, 3):
                        seq = blk // 2
                        o.dma("pool", dv(O["ns"][seq, l, 0].rearrange("h d v -> d h v"), "ns", seq, l, 0),
                              Sd.re("p (h v) -> p h v", h=4), key=("sout", 0))
                    if d == 1 and blk in (2, 0):
                        seq = blk // 2
                        o.dma("pool", dv(O["ns"][seq, l, 1].rearrange("h d v -> d h v"), "ns", seq, l, 1),
                              Sd.re("p (h v) -> p h v", h=4), key=("sout", 1))
                else:
                    if d == 1 and blk == 0:
                        o.dma("pool", dv(S["sp_U"][si], "sp_U", si), Sd, key=("sout", 1))

            def dir_gen(d):
                blocks = (0, 1, 2, 3) if d == 0 else (3, 2, 1, 0)
                chunks = (0, 1, 2, 3) if d == 0 else (3, 2, 1, 0)
                for b in blocks:
                    state_init(d, b)
                    kslot, asl = block_setup(d, b)
                    yield
                    pending = []
                    for ci, c in enumerate(chunks):
                        last = (ci == 3)
                        sbf = chunk_a(d, b, c, kslot, last)
                        pending.append((d, b, c, sbf, last))
                        yield
                        if ci == 0:
                            block_setup_b(d, b, asl)
                        if len(pending) > 2:
                            chunk_b(*pending.pop(0))
                    chunk_b(*pending.pop(0))
                    yield
                    chunk_b(*pending.pop(0))
                    block_finish(d, b)
                    state_out(d, b)
                    yield

            gens = [dir_gen(0), dir_gen(1)]
            while gens:
                for g_ in list(gens):
                    try:
                        next(g_)
                    except StopIteration:
                        gens.remove(g_)
                        continue
                    yield
            o.dma("pool", dv(S["sp_o"][si], "sp_o", si),
                  V(o_f.h[:, :, :], [("o_f", b_) for b_ in range(4)]), key=("spo",))
            yield

        prev_attn = None
        prev_chain = None
        gblock = 0
        for sgi, sg in enumerate(segs):
            kind = sg["kind"]
            si = sg["idx"]
            cond = sg["cond"]
            rope = kind == "s"
            def stage1(b):
                gi = sgi * 4 + b
                slot = gi % 2
                xv = xin[:, slot, :].k(slot)
                st_ = stat_ring.get()
                junk = tmp_ring.get().bitcast(BF16)
                o.memset("dve", st_[:, 0:2], 0.0)
                o.act(junk, xv[:, 0:1024], AF.Square, accum=st_[:, 0:1])
                o.act(junk, xv[:, 1024:2048], AF.Square, accum=st_[:, 1:2])
                o.tt("dve", st_[:, 2:3], st_[:, 0:1], st_[:, 1:2], ALU.add)
                o.act(st_[:, 3:4], st_[:, 2:3], AF.Ln, scale=1.0 / D, bias=EPS)
                o.act(st_[:, 4:5], st_[:, 3:4], AF.Exp, scale=-0.5)
                o.ts("dve", xv, xv, st_[:, 4:5], 0.0, ALU.mult, ALU.add)

            def stage2(b):
                gi = sgi * 4 + b
                slot = gi % 2
                xv = xin[:, slot, :].k(slot)
                for k4 in range(4):
                    pt = bank(short_ring.get())
                    for j in range(4):
                        kc = k4 * 4 + j
                        o.tr(pt[:, j * 128:(j + 1) * 128], xv[:, kc * 128:(kc + 1) * 128], ident_f)
                    for j in range(4):
                        kc = k4 * 4 + j
                        dst = hT[:, kc, b * 128:(b + 1) * 128].k(kc)
                        gcol = g1[:, l, cond, kc:kc + 1]
                        scol = mod[:, l, cond, kc:kc + 1]
                        if k4 % 2 == 0:
                            o.act(dst, pt[:, j * 128:(j + 1) * 128], AF.Identity, scale=gcol, bias=scol)
                        else:
                            o.ts("dve", dst, pt[:, j * 128:(j + 1) * 128], gcol, scol, ALU.mult, ALU.add)
                issue_xload(gi + 2)

            stage1(0)
            for b in range(4):
                if b + 1 < 4:
                    stage1(b + 1)
                stage2(b)
                pump(1)
            if "hT" in dbg and l == 0 and sgi == 0:
                o.dma("sp", V(dbg["hT"], [("dram", "dbg_hT")]), V(hT.h[:, :, :], [("hT", kc_) for kc_ in range(16)]),
                      key=("dbg",))
            maybe_stop(P, "p1N%d" % sgi)
            if rope:
                r0 = sg["row0"]
                o.dma("sp", cs[:, :], dv(I["cosT"][:, r0:r0 + SEG], "cosT"), key=("cs",))
                o.dma("sp", sn[:, :], dv(I["sinT"][:, r0:r0 + SEG], "sinT"), key=("sn",))

            def load_w(G):
                slot = state["wload_i"] % 3
                state["wload_i"] += 1
                wv = wt[:, slot, :, :].k(slot)
                src = S["wbf"][l].rearrange("(kc p) n -> p kc n", p=128)[:, :, G * 256:(G + 1) * 256]
                o.dma("sp", wv, V(src, [("dram", "wbf", l)]), key=("wt", slot))
                return wv

            def fm_block(wv, j):
                pb = bank(acc_ring.get())
                for kc in range(16):
                    o.mm(pb, lhsT=wv[:, kc, j * 128:(j + 1) * 128], rhs=hT[:, kc, :].k(kc),
                         start=(kc == 0), stop=(kc == 15))
                return pb

            def tm_block(wv, b, pb, c0):
                for kc in range(16):
                    o.mm(pb[:, c0:c0 + 256], lhsT=hT[:, kc, b * 128:(b + 1) * 128].k(kc), rhs=wv[:, kc, :],
                         start=(kc == 0), stop=(kc == 15))

            def qk_prep2(pbs, gcol, is_k, heads):
                n = len(pbs)
                sqb = [tmp_ring.get().bitcast(BF16)[:, 0:512] for _ in range(n)]
                qg_ = [tmp_ring.get() for _ in range(n)]
                for i in range(n):
                    o.act(sqb[i], pbs[i], AF.Square)
                    o.act(qg_[i], pbs[i], AF.Identity, scale=gcol)
                ss = [bank(short_ring.get()) for _ in range(n)]
                for i in range(n):
                    o.mm(ss[i], lhsT=ones_b, rhs=sqb[i])
                rs = [rstd_from(ss[i], 128.0) for i in range(n)]
                dests = []
                for i in range(n):
                    if is_k:
                        dests.append(kTb[:, heads[i], 2:6, :].k(heads[i], "cur"))
                    else:
                        dests.append(qcur[:, :, heads[i], :].k(heads[i]))
                if rope:
                    qnb = [tmp_ring.get().bitcast(BF16)[:, 0:512] for _ in range(n)]
                    for i in range(n):
                        o.tt("dve", qnb[i], qg_[i], rs[i], ALU.mult)
                    rot = [bank(short_ring.get()) for _ in range(n)]
                    t1_ = [tmp_ring.get() for _ in range(n)]
                    for i in range(n):
                        o.mm(rot[i], lhsT=C["rot_m"][:, :], rhs=qnb[i])
                        o.tt("pool", t1_[i], qnb[i], cs[:, :], ALU.mult)
                    for i in range(n):
                        t2_ = sqb[i].bitcast(F32) if False else qg_[i]
                        o.tt("dve", t2_, rot[i], sn[:, :], ALU.mult)
                        o.tt("dve", dests[i], t1_[i].re("p (b t) -> p b t", b=4), t2_.re("p (b t) -> p b t", b=4),
                             ALU.add)
                else:
                    for i in range(n):
                        head = heads[i]
                        if is_k:
                            qn = tmp_ring.get()
                            o.tt("dve", qn, qg_[i], rs[i], ALU.mult)
                            o.copy("act", dests[i], qn.re("p (b t) -> p b t", b=4))
                            for b in range(4):
                                pt = bank(short_ring.get())
                                o.tr(pt[:, 0:128], qn[:, b * 128:(b + 1) * 128], ident_f)
                                tk_ = tmp_ring.get()
                                o.copy("act", tk_[:, 0:128], pt[:, 0:128])
                                seq, tb = b // 2, b % 2
                                o.dma("pool", dv(O["nk"][seq, l, head, tb * 128:(tb + 1) * 128, :], "nk", seq, l, head, tb),
                                      tk_[:, 0:128], key=("nk", b % 2))
                        else:
                            o.tt("dve", dests[i], qg_[i].re("p (b t) -> p b t", b=4),
                                 rs[i].re("p (b t) -> p b t", b=4), ALU.mult)

            for G in range(26):
                if OPTS["stop"] == "p1s%dG%d" % (sgi, G) and l == 0:
                    drain_all()
                    maybe_stop(P, OPTS["stop"])
                if G == 0 and prev_attn is not None:
                    drain_task(prev_attn)
                    prev_attn = None
                if G == 16 and prev_chain is not None:
                    drain_task(prev_chain)
                    prev_chain = None
                wv = load_w(G)
                if G == 0:
                    pbs = [fm_block(wv, j) for j in range(2)]
                    qk_prep2(pbs, kgcol, True, [0, 1])
                    pump(4)
                elif G == 1:
                    for b in range(4):
                        pb = bank(acc_ring.get())
                        tm_block(wv, b, pb, 0)
                        o.copy("act", vtb[:, 2 + b, :].k("cur", b), pb[:, 0:256])
                        if kind == "p":
                            tv = tmp_ring.get()
                            o.copy("dve", tv[:, 0:256], pb[:, 0:256])
                            seq, tb = b // 2, b % 2
                            o.dma("pool", dv(O["nv"][seq, l, :, tb * 128:(tb + 1) * 128, :].rearrange("h t d -> t h d"),
                                             "nv", seq, l, tb),
                                  tv[:, 0:256].re("p (h d) -> p h d", h=2), key=("nv", b % 2))
                        pump(1)
                elif 2 <= G <= 5:
                    pbs = [fm_block(wv, j) for j in range(2)]
                    qk_prep2(pbs, qgcol, False, [(G - 2) * 2, (G - 2) * 2 + 1])
                    pump(4)
                elif 6 <= G <= 9:
                    for j in range(2):
                        head = (G - 6) * 2 + j
                        pb = fm_block(wv, j)
                        silu_to(mixA[:, :, head, :].k(head), pb, blocked=True)
                        pump(2)
                    if G == 9:
                        task = attention_task(sg, attn_blocks(sg))
                        queue.append(task)
                        prev_attn = task
                elif G in (10, 11):
                    if G == 10:
                        state["cv_w0"] = wv
                    else:
                        w0 = state["cv_w0"]
                        for b in range(4):
                            pb = bank(acc_ring.get())
                            tm_block(w0, b, pb, 0)
                            tm_block(wv, b, pb, 256)
                            st_ = stat_ring.get()
                            o.bn_stats(st_[:, 0:6], pb)
                            o.bn_aggr(st_[:, 6:8], st_[:, 0:6])
                            st2 = stat_ring.get()
                            o.act(st2[:, 0:1], st_[:, 7:8], AF.Ln, bias=EPS)
                            o.act(st2[:, 1:2], st2[:, 0:1], AF.Exp, scale=-0.5)
                            tn = tmp_ring.get()
                            o.ts("dve", tn, pb, st_[:, 6:7], st2[:, 1:2], ALU.subtract, ALU.mult)
                            o.tt("dve", tn, tn, lngb[:, :], ALU.mult)
                            o.tt("dve", vn_tm[:, b, :].k(b), tn, lnbb[:, :], ALU.add)
                            pump(2)
                elif G in (12, 13):
                    for j in range(2):
                        gg = (G - 12) * 2 + j
                        pb = fm_block(wv, j)
                        silu_to(scg[:, gg, :].k(gg), pb)
                        pump(2)
                elif G in (14, 15):
                    for j in range(2):
                        gg = (G - 14) * 2 + j
                        pb = fm_block(wv, j)
                        sps = bank(short_ring.get())
                        for b in range(4):
                            o.mm(sps[:, b * 128:(b + 1) * 128], lhsT=vn_tm[:, b, gg * 128:(gg + 1) * 128].k(b),
                                 rhs=wsT[:, gg, :].k(gg), start=True, stop=False)
                            o.mm(sps[:, b * 128:(b + 1) * 128], lhsT=ones_f[0:1, 0:128],
                                 rhs=bsrow[0:1, gg * 128:(gg + 1) * 128], start=False, stop=True)
                        s_sb = tmp_ring.get()
                        o.copy("act", s_sb, sps)
                        t_ = tmp_ring.get()
                        o.tt("dve", t_, pb, s_sb, ALU.mult)
                        o.tt("dve", mixC[:, gg, :].k(gg), t_, scg[:, gg, :].k(gg), ALU.mult)
                        pump(2)
                    if G == 15:
                        o.dma("pool", dv(S["sp_mixC"][si], "sp_mixC", si),
                              V(mixC.h[:, :, :], [("mixC", g_) for g_ in range(4)]), key=("spc",))
                elif G in (16, 17):
                    if G == 16:
                        state["bi_w0"] = wv
                    else:
                        w0 = state["bi_w0"]
                        for b in range(4):
                            pb = bank(acc_ring.get())
                            tm_block(w0, b, pb, 0)
                            tm_block(wv, b, pb, 256)
                            o.copy("act", vtmb[:, b, :].k(b), pb)
                            pump(2)
                else:
                    h = (G - 18) // 2
                    lbcol = lbv[:, l, :, h]
                    omlcol = oml[:, l, :, h]

                    def gates(pb, d):
                        e_ = tmp_ring.get()
                        o.act(e_, pb, AF.Exp, scale=-1.0)
                        o.act(e_, e_, AF.Ln, bias=1.0)
                        o.act(e_, e_, AF.Exp, scale=-1.0)
                        f_ = tmp_ring.get()
                        o.ts("dve", f_, e_, omlcol[:, d:d + 1], lbcol[:, d:d + 1], ALU.mult, ALU.add)
                        lf_ = tmp_ring.get()
                        o.act(lf_, f_, AF.Ln)
                        k_ = e_
                        o.ts("pool", k_, f_, -1.0, 1.0, ALU.mult, ALU.add)
                        return lf_, k_

                    if (G - 18) % 2 == 0:
                        pbq = fm_block(wv, 0)
                        silu_to(qs[:, :], pbq)
                        pump(1)
                        pbf = fm_block(wv, 1)
                        lf_, k_ = gates(pbf, 0)
                        c_ = tmp_ring.get()
                        o.scan(c_, C["scanmask"][:, :], lf_)
                        ec = tmp_ring.get()
                        o.act(ec, c_, AF.Exp)
                        en = lf_
                        o.act(en, c_, AF.Exp, scale=-1.0)
                        o.tt("dve", qe[:, 0, h, :].k(0, h), qs[:, :], ec, ALU.mult)
                        o.tt("dve", keT[:, 0, h, :].k(0, h), k_, en, ALU.mult)
                        o.copy("dve", dec[:, 0, h, :].k(0), ec.re("p (c j) -> p c j", j=32)[:, :, 31])
                        pump(2)
                    else:
                        pbb = fm_block(wv, 0)
                        lf_, k_ = gates(pbb, 1)
                        c_ = tmp_ring.get()
                        o.scan(c_, C["scanmask"][:, :], lf_)
                        o.act(dec[:, 1, h, :].k(1), c_.re("p (c j) -> p c j", j=32)[:, :, 31], AF.Exp)
                        cx = tmp_ring.get()
                        o.tt("pool", cx, c_, lf_, ALU.subtract)
                        ex = c_
                        o.act(ex, cx, AF.Exp)
                        enx = tmp_ring.get()
                        o.act(enx, cx, AF.Exp, scale=-1.0)
                        o.tt("dve", qe[:, 1, h, :].k(1, h), qs[:, :], enx, ALU.mult)
                        o.tt("dve", keT[:, 1, h, :].k(1, h), k_, ex, ALU.mult)
                        if kind == "s":
                            cseg = tmp_ring.get()
                            o.scan(cseg, onecol[:, 0:1].bc([128, 512]), lf_)
                            st_ = stat_ring.get()
                            o.copy("dve", st_[:, 0:1], cseg[:, 511:512])
                            o.act(dseg[:, si, h:h + 1].k(si, h), st_[:, 0:1], AF.Exp)
                            o.tt("dve", cx, cseg, lf_, ALU.subtract)
                            dg = enx
                            o.act(dg, cx, AF.Exp, scale=-1.0, bias=st_[:, 0:1])
                            qd = tmp_ring.get().bitcast(BF16)[:, 0:512]
                            o.tt("dve", qd, qs[:, :], dg, ALU.mult)
                            o.dma("pool", dv(S["sp_qd"][si, :, h, :], "sp_qd", si, h), qd, key=("spq", h % 2))
                        pump(1)
                        pbg = fm_block(wv, 1)
                        sbg = tmp_ring.get().bitcast(BF16)[:, 0:512]
                        silu_to(sbg, pbg)
                        o.dma("pool", dv(S["sp_sbg"][si, :, h, :], "sp_sbg", si, h), sbg, key=("sps", h % 2))
                        pump(2)
                    if G == 25:
                        task = chain_task(sg)
                        queue.append(task)
                        prev_chain = task
        drain_all()
        if l == 0:
            for nm_, src_ in (("mixA", "sp_mixA"), ("mixC", "sp_mixC"), ("spo", "sp_o")):
                if nm_ in dbg:
                    for si_ in range(NSEG):
                        o.dma("sp", V(dbg[nm_][si_], [("dram", "dbg", nm_, si_)]),
                              V(S[src_][si_], [("dram", src_, si_)] + [("dram", src_, si_, x_) for x_ in ("cur", "lag")]),
                              key=("dbg",))
        P.emit_block()
        if OPTS["stop"] == "p1":
            P.dead = True


def build_pass2(nc, P, o, l, NS, I, O, S, C, small, dseg, psb, xsrc_s, xsrc_p, xdst_s, xdst_p,
                xkey_in, xkey_out, dbg, ADA):
    NSEG = NS + 1

    def dv(ap, *key):
        return V(ap, [("dram",) + tuple(key)])

    def bank(i):
        return psb[i][:, :]

    acc_ring = Ring([0, 1, 2, 3])
    next_ada = (l + 1 < DEPTH)
    short_ring = Ring([4, 5, 6] if next_ada else [4, 5, 6, 7])

    with ExitStack() as es2:
        def sb(name, shape, dt):
            return Buf(es2.enter_context(nc.sbuf_tensor(name + "_%d" % l, list(shape), dt)), name)

        wo = sb("wo", [128, 16, D], BF16)
        gate = sb("gate", [128, 2, D], F32)
        sinb = sb("sinb", [128, NS, 512], BF16)
        sin_f = sb("sin_f", [128, 2, 512], F32)
        mA = sb("mA", [128, 2, 4, 8, 128], BF16)
        mC = sb("mC", [128, 2, 4, 512], BF16)
        mB = sb("mB", [128, 4, 512], BF16)
        ol = sb("ol", [128, 4, 512], F32)
        qd2 = sb("qd2", [128, 4, 512], BF16)
        sbg2 = sb("sbg2", [128, 4, 512], BF16)
        NXR = 2 if next_ada else 3
        xr = sb("xr", [128, NXR, D], F32)
        NT = 6 if next_ada else 10
        tmpb = sb("tmp2", [128, NT, 512], F32)
        ada_task = None
        if next_ada:
            wa2 = sb("wa2", [128, 2, 3 * D], BF16)
            rowsT2 = sb("rowsT2", [16, 128], F32)
            t02 = sb("t02", [128, 16], F32)
            ada_task = adaln_gen(o, l + 1, I, S, C, ADA, wa2, bank(7), lambda: bank(short_ring.get()), rowsT2, t02)
        tmp_ring = Ring([tmpb[:, i, :].k(i) for i in range(NT)])
        ones_b = C["ones_b"][:, :]
        hgcol = small[:, 2, l:l + 1]

        wsrc = S["wobf"][l].rearrange("(kc p) n -> p kc n", p=128)
        for i in range(4):
            o.dma("sp", wo[:, 4 * i:4 * i + 4, :].k(i), V(wsrc[:, 4 * i:4 * i + 4, :], [("dram", "wobf", l)]),
                  key=("wo", i))
        for cond in range(2):
            o.dma("sp", gate[:, cond, :].k(cond),
                  dv(S["gscr"][l, cond:cond + 1].rearrange("o k f -> o (k f)").partition_broadcast(128), "gscr"),
                  key=("gate", cond))

        cur = sin_f[:, 0, :].k(0)
        o.dma("sp", cur.re("p (h v) -> p h v", h=4), dv(I["st"][l, 1].rearrange("h d v -> d h v"), "st"),
              key=("sinit2",))
        o.copy("act", sinb[:, NS - 1, :].k(NS - 1), cur)
        for s_ in range(NS - 2, -1, -1):
            si_next = s_ + 2
            ut = tmp_ring.get()
            o.dma("sp", ut, dv(S["sp_U"][si_next], "sp_U"), key=("uld",))
            nxt = sin_f[:, (NS - 1 - s_) % 2, :].k((NS - 1 - s_) % 2)
            o.tt("dve", nxt.re("p (h v) -> p h v", h=4), cur.re("p (h v) -> p h v", h=4),
                 dseg[:, si_next, :].re("p (h o) -> p h o", o=1).bc([128, 4, 128]), ALU.mult)
            o.tt("dve", nxt, nxt, ut, ALU.add)
            o.copy("act", sinb[:, s_, :].k(s_), nxt)
            cur = nxt

        segs = [dict(kind="p", idx=0, src=xsrc_p, dst=xdst_p, row0=0, cond=1)]
        for s_ in range(NS):
            segs.append(dict(kind="s", idx=s_ + 1, src=xsrc_s, dst=xdst_s, row0=s_ * SEG, cond=0, s=s_))
        allblocks = [(sg, b) for sg in segs for b in range(4)]

        def issue_xload(gi):
            if gi >= len(allblocks):
                return
            sg, b = allblocks[gi]
            slot = gi % NXR
            r0 = sg["row0"] + b * 128
            o.dma("sp", xr[:, slot, :].k(slot), dv(sg["src"][r0:r0 + 128, :], *xkey_in), key=("xr", slot))

        def issue_segload(sgi):
            if sgi >= len(segs):
                return
            sg = segs[sgi]
            si = sg["idx"]
            slot = sgi % 2
            o.dma("sp", mA[:, slot, :, :, :].k(slot), dv(S["sp_mixA"][si], "sp_mixA"), key=("mA", slot))
            o.dma("sp", mC[:, slot, :, :].k(slot), dv(S["sp_mixC"][si], "sp_mixC"), key=("mC", slot))

        issue_segload(0)
        for gi_ in range(NXR - 1):
            issue_xload(gi_)
        for sgi, sg in enumerate(segs):
            si = sg["idx"]
            kind = sg["kind"]
            cond = sg["cond"]
            slot = sgi % 2
            def ada_step():
                nonlocal ada_task
                if ada_task is not None:
                    try:
                        next(ada_task)
                    except StopIteration:
                        ada_task = None
            ada_step()
            o.dma("sp", ol[:, :, :], dv(S["sp_o"][si], "sp_o"), key=("ol",))
            o.dma("sp", sbg2[:, :, :], dv(S["sp_sbg"][si], "sp_sbg"), key=("sbg2",))
            if kind == "s":
                o.dma("sp", qd2[:, :, :], dv(S["sp_qd"][si], "sp_qd"), key=("qd2",))
            issue_segload(sgi + 1)
            for h in range(4):
                if kind == "s":
                    pc = bank(short_ring.get())
                    o.mm(pc, lhsT=sinb[:, sg["s"], h * 128:(h + 1) * 128].k(sg["s"]), rhs=qd2[:, h, :])
                    o_ = tmp_ring.get()
                    o.tt("dve", o_, ol[:, h, :], pc, ALU.add)
                else:
                    o_ = ol[:, h, :]
                sqb = tmp_ring.get().bitcast(BF16)[:, 0:512]
                o.act(sqb, o_, AF.Square)
                ss = bank(short_ring.get())
                o.mm(ss, lhsT=ones_b, rhs=sqb)
                lnv = tmp_ring.get()
                o.act(lnv, ss, AF.Ln, scale=1.0 / 128.0, bias=EPS)
                rs = tmp_ring.get()
                o.act(rs, lnv, AF.Exp, scale=-0.5)
                on = tmp_ring.get()
                o.stt("dve", on, o_, hgcol, rs, ALU.mult, ALU.mult)
                o.tt("dve", mB[:, h, :].k(h), on, sbg2[:, h, :], ALU.mult)
            if "mB" in dbg and l == 0:
                o.dma("pool", V(dbg["mB"][si], [("dram", "dbg_mB", si)]), V(mB.h[:, :, :], [("mB", h_) for h_ in range(4)]),
                      key=("dbgmb",))
            for b in range(4):
                gi = sgi * 4 + b
                xslot = gi % NXR
                xv = xr[:, xslot, :].k(xslot)
                for n in range(4):
                    pb = bank(acc_ring.get())
                    for kc in range(16):
                        if kc < 8:
                            lhsT = mA[:, slot, b, kc, :].k(slot)
                        elif kc < 12:
                            lhsT = mB[:, kc - 8, b * 128:(b + 1) * 128].k(kc - 8)
                        else:
                            lhsT = mC[:, slot, kc - 12, b * 128:(b + 1) * 128].k(slot)
                        o.mm(pb, lhsT=lhsT, rhs=wo[:, kc, n * 512:(n + 1) * 512].k(kc // 4),
                             start=(kc == 0), stop=(kc == 15))
                    t_ = tmp_ring.get()
                    o.tt("dve", t_, pb, gate[:, cond, n * 512:(n + 1) * 512].k(cond), ALU.mult)
                    o.tt("dve", xv[:, n * 512:(n + 1) * 512], t_, xv[:, n * 512:(n + 1) * 512], ALU.add)
                r0 = sg["row0"] + b * 128
                o.dma("pool", dv(sg["dst"][r0:r0 + 128, :], *xkey_out), xv, key=("yout", xslot))
                issue_xload(gi + NXR - 1)
                if b == 1:
                    ada_step()
                    ada_step()
                if b == 3:
                    ada_step()
        if ada_task is not None:
            for _ in ada_task:
                pass
        P.emit_block()


_PROG_CACHE = {}


def _get_prog(NS):
    if NS not in _PROG_CACHE:
        _PROG_CACHE[NS] = build_program(NS)
    return _PROG_CACHE[NS]


def run_cores(per_core_inputs, NS):
    nc = _get_prog(NS)
    res = run_bass_kernel_spmd(nc, per_core_inputs, core_ids=list(range(len(per_core_inputs))))
    return res.results


def make_core_inputs(i, NS, x_prompt, x_sample, cache_k, cache_v, state_hgrn, c, c_ctx, shared):
    d = dict(shared)
    d["xs"] = np.ascontiguousarray(x_sample[i])
    d["xp"] = np.ascontiguousarray(x_prompt[2 * i:2 * i + 2].reshape(512, D))
    d["ck"] = np.ascontiguousarray(cache_k[i])
    d["cv"] = np.ascontiguousarray(cache_v[i])
    d["st"] = np.ascontiguousarray(state_hgrn[i])
    d["cvec"] = np.ascontiguousarray(np.stack([c[i], c_ctx], 0))
    return d


def make_shared(NS, norm_g, w_ada, b_ada, w_in, q_norm_g, k_norm_g, attn_sink, hgrn_lb, hgrn_norm_g,
                sgu_norm_g, sgu_norm_b, sgu_w, sgu_b, w_out):
    f = lambda a: np.ascontiguousarray(np.asarray(a, dtype=np.float32))
    sh = dict(norm_g=f(norm_g), w_ada=f(w_ada), b_ada=f(b_ada),
              w_in=np.ascontiguousarray(np.asarray(w_in, dtype=np.float32)[:, :, w_in_perm()]),
              qg=f(q_norm_g), kg=f(k_norm_g), sink=f(attn_sink), lb=f(hgrn_lb), hg=f(hgrn_norm_g),
              lng=f(sgu_norm_g), lnb=f(sgu_norm_b), sgu_w=f(sgu_w), sgu_b=f(sgu_b), w_out=f(w_out))
    sh.update(host_consts(NS * SEG))
    return sh


def kernel(x_prompt, x_sample, cache_k, cache_v, state_hgrn, c, c_ctx, norm_g, w_ada, b_ada, w_in,
           q_norm_g, k_norm_g, attn_sink, hgrn_lb, hgrn_norm_g, sgu_norm_g, sgu_norm_b, sgu_w, sgu_b, w_out):
    x_prompt = np.asarray(x_prompt, dtype=np.float32)
    x_sample = np.asarray(x_sample, dtype=np.float32)
    cache_k = np.asarray(cache_k, dtype=np.float32)
    cache_v = np.asarray(cache_v, dtype=np.float32)
    state_hgrn = np.asarray(state_hgrn, dtype=np.float32)
    c = np.asarray(c, dtype=np.float32)
    c_ctx = np.asarray(c_ctx, dtype=np.float32)
    NB = x_sample.shape[0]
    NS = x_sample.shape[1] // SEG
    shared = make_shared(NS, norm_g, w_ada, b_ada, w_in, q_norm_g, k_norm_g, attn_sink, hgrn_lb, hgrn_norm_g,
                         sgu_norm_g, sgu_norm_b, sgu_w, sgu_b, w_out)
    ins = [make_core_inputs(i, NS, x_prompt, x_sample, cache_k, cache_v, state_hgrn, c, c_ctx, shared)
           for i in range(NB)]
    res = run_cores(ins, NS)
    y_prompt = np.concatenate([r["yp"].reshape(2, 256, D) for r in res], 0).astype(np.float32)
    y_sample = np.stack([r["ys"] for r in res], 0).astype(np.float32)
    nk = np.concatenate([r["nk"] for r in res], 0).astype(np.float32)
    nv = np.concatenate([r["nv"] for r in res], 0).astype(np.float32)
    ns = np.concatenate([r["ns"] for r in res], 0).astype(np.float32)
    return (y_prompt, y_sample, nk, nv, ns)
```
